# Optimizing a Trainium2 kernel written in Bass

```python
import jax
import jax.numpy as jnp
from jax import lax
import numpy as np

D_MODEL = 2048
BATCH = 4
SEQ = 4096
DEPTH = 2

GRID_W = 64
CTX_LEN = 256
HEAD_DIM = 128
EPS = 1e-6
N_MOD = 9
N_BRANCH = 3
FFN_DIM = 256 * ((8 * D_MODEL // 3 + 255) // 256)
A_GROUPS = D_MODEL // (4 * HEAD_DIM)
A_CHUNK = 128
A_WIDTH = A_GROUPS * HEAD_DIM
B_HEADS = D_MODEL // (4 * HEAD_DIM)
B_DK = HEAD_DIM // 2
B_DV = HEAD_DIM
B_KW = B_HEADS * B_DK
B_VW = B_HEADS * B_DV
B_RANK = 16
B_TAU = 16.0
B_CHUNK = 64
C_HEADS = D_MODEL // (2 * HEAD_DIM)
C_KV_HEADS = C_HEADS // 4
C_QW = C_HEADS * HEAD_DIM
C_KVW = C_KV_HEADS * HEAD_DIM
C_WINDOW = 128
C_BLOCK = 128
ROPE_BASE = 10000.0
COL_BK = 0
COL_BV = COL_BK + B_KW
COL_CK = COL_BV + B_VW
COL_CV = COL_CK + C_KVW
CTX_STATE_COLS = COL_CV + C_KVW
COL_AU = CTX_STATE_COLS
COL_AV = COL_AU + A_WIDTH
COL_BQ = COL_AV + A_WIDTH
COL_BG = COL_BQ + B_KW
COL_CQ = COL_BG + B_VW
COL_GATE = COL_CQ + C_QW
IN_COLS = COL_GATE + N_BRANCH * D_MODEL

kernel_name = 'hybrid_gmlp_gla_swa_macaron_dit'


def rms_norm(x, gain):
    xf = x.astype(jnp.float32)
    y = xf * lax.rsqrt(jnp.mean(xf * xf, axis=-1, keepdims=True) + EPS)
    return (y * gain.astype(jnp.float32)).astype(x.dtype)


def layer_norm(x, gain):
    xf = x.astype(jnp.float32)
    mu = jnp.mean(xf, axis=-1, keepdims=True)
    var = jnp.mean(jnp.square(xf - mu), axis=-1, keepdims=True)
    return ((xf - mu) * lax.rsqrt(var + EPS) * gain.astype(jnp.float32)).astype(x.dtype)


def modulate(h, shift, scale):
    return h * (1 + scale) + shift


def swiglu(h, w_up, w_down):
    gate, val = jnp.split(h @ w_up, 2, axis=-1)
    return (jax.nn.silu(gate) * val) @ w_down


def cols(z, start, width):
    return z[..., start:start + width]


def heads(z, start, width, n, d):
    return cols(z, start, width).reshape(z.shape[0], z.shape[1], n, d)


def flip(t):
    return jnp.flip(t, axis=1)


def rope_axis(x, pos):
    half = x.shape[-1] // 2
    inv_freq = ROPE_BASE ** (-jnp.arange(half, dtype=jnp.float32) / half)
    ang = pos.astype(jnp.float32)[:, None] * inv_freq[None, :]
    cos = jnp.cos(ang)[:, None, :]
    sin = jnp.sin(ang)[:, None, :]
    xf = x.astype(jnp.float32)
    x1, x2 = xf[..., :half], xf[..., half:]
    return jnp.concatenate([x1 * cos - x2 * sin, x1 * sin + x2 * cos], axis=-1).astype(x.dtype)


def rope_2d(x):
    L = x.shape[1]
    rows = L // GRID_W
    row = jnp.repeat(jnp.arange(rows), GRID_W)
    col = jnp.tile(jnp.arange(GRID_W), rows)
    ax = x.shape[-1] // 2
    return jnp.concatenate([rope_axis(x[..., :ax], row), rope_axis(x[..., ax:], col)], axis=-1)


def chunk_spatial_gating(u, v, v_gain, w_s, b_s):
    B_, L, _ = u.shape
    u = jax.nn.gelu(u)
    v = layer_norm(jax.nn.gelu(v), v_gain)
    vr = v.reshape(B_, L // A_CHUNK, A_CHUNK, A_GROUPS, HEAD_DIM)
    mixed = jnp.einsum('gpq,bnqgd->bnpgd', w_s, vr) + b_s.T[None, None, :, :, None]
    return u * mixed.reshape(B_, L, A_WIDTH)


def gla_log_decay(h, w1, w2, bias):
    logit = ((h @ w1) @ w2 + bias).astype(jnp.float32)
    return (jax.nn.log_sigmoid(logit) / B_TAU).reshape(h.shape[0], h.shape[1], B_HEADS, B_DK)


def gla_chunked(q, k, v, g, s0):
    B_, L, H, _ = q.shape
    n = L // B_CHUNK

    def to_chunks(t):
        return t.reshape(B_, n, B_CHUNK, H, t.shape[-1]).transpose(1, 0, 3, 2, 4)

    qc, kc, vc, gc = to_chunks(q), to_chunks(k), to_chunks(v), to_chunks(g)
    bc = jnp.cumsum(gc, axis=3)
    b_last = bc[:, :, :, -1:, :]
    q_in = qc * jnp.exp(bc)
    k_in = kc * jnp.exp(-bc)
    k_out = kc * jnp.exp(b_last - bc)
    mask = jnp.tril(jnp.ones((B_CHUNK, B_CHUNK), dtype=bool))
    attn = jnp.where(mask, jnp.einsum('nbhtd,nbhsd->nbhts', q_in, k_in), 0.0)
    intra = jnp.einsum('nbhts,nbhsv->nbhtv', attn, vc)

    def step(s, inp):
        qi, ko, vi, bl = inp
        o = jnp.einsum('bhtd,bhdv->bhtv', qi, s)
        s = s * jnp.exp(bl[:, :, 0, :])[..., None] + jnp.einsum('bhsd,bhsv->bhdv', ko, vi)
        return s, o

    s_fin, inter = lax.scan(step, s0, (q_in, k_out, vc, b_last))
    o = (intra + inter).transpose(1, 0, 3, 2, 4).reshape(B_, L, H, v.shape[-1])
    return o, s_fin


def gla_final_state(k, v, g):
    b = jnp.cumsum(g, axis=1)
    w = jnp.exp(b[:, -1:] - b)
    return jnp.einsum('blhd,blhv->bhdv', k * w, v)


def gla_output(o, og, gain):
    o = rms_norm(o, gain).reshape(o.shape[0], o.shape[1], B_VW).astype(og.dtype)
    return o * jax.nn.silu(og)


def window_attention(q, k, v, k_ctx, v_ctx, sink):
    B_, L, H, D = q.shape
    KV = k.shape[2]
    G = H // KV
    nb = L // C_BLOCK
    scale = D ** -0.5
    qb = q.reshape(B_, nb, C_BLOCK, KV, G, D)
    pad = ((0, 0), (C_BLOCK, C_BLOCK), (0, 0), (0, 0))

    def band(t):
        tp = jnp.pad(t, pad).reshape(B_, nb + 2, C_BLOCK, KV, D)
        return jnp.concatenate([tp[:, :-2], tp[:, 1:-1], tp[:, 2:]], axis=2)

    kb, vb = band(k), band(v)
    s_loc = jnp.einsum('bnqkgd,bnjkd->bkgnqj', qb, kb).astype(jnp.float32) * scale
    qpos = jnp.arange(nb)[:, None] * C_BLOCK + jnp.arange(C_BLOCK)[None, :]
    kpos = jnp.arange(nb)[:, None] * C_BLOCK - C_BLOCK + jnp.arange(3 * C_BLOCK)[None, :]
    valid = ((jnp.abs(qpos[:, :, None] - kpos[:, None, :]) <= C_WINDOW)
             & (kpos[:, None, :] >= 0) & (kpos[:, None, :] < L))
    s_loc = jnp.where(valid, s_loc, -jnp.inf)
    s_ctx = jnp.einsum('bnqkgd,bjkd->bkgnqj', qb, k_ctx).astype(jnp.float32) * scale
    s_sink = jnp.broadcast_to(sink.astype(jnp.float32).reshape(KV, G)[None, :, :, None, None, None],
                              s_loc.shape[:-1] + (1,))
    p = jax.nn.softmax(jnp.concatenate([s_loc, s_ctx, s_sink], axis=-1), axis=-1)
    n_loc = 3 * C_BLOCK
    n_ctx = k_ctx.shape[1]
    o = (jnp.einsum('bkgnqj,bnjkd->bnqkgd', p[..., :n_loc].astype(v.dtype), vb)
         + jnp.einsum('bkgnqj,bjkd->bnqkgd', p[..., n_loc:n_loc + n_ctx].astype(v.dtype), v_ctx))
    return o.reshape(B_, L, H * D)


def context_attention(q, k, v, sink):
    B_, L, H, D = q.shape
    KV = k.shape[2]
    G = H // KV
    qg = q.reshape(B_, L, KV, G, D)
    s = jnp.einsum('bqkgd,bjkd->bkgqj', qg, k).astype(jnp.float32) * D ** -0.5
    s_sink = jnp.broadcast_to(sink.astype(jnp.float32).reshape(KV, G)[None, :, :, None, None],
                              s.shape[:-1] + (1,))
    p = jax.nn.softmax(jnp.concatenate([s, s_sink], axis=-1), axis=-1)[..., :-1]
    o = jnp.einsum('bkgqj,bjkd->bqkgd', p.astype(v.dtype), v)
    return o.reshape(B_, L, H * D)


def branch_merge(z, a, b, c, w_br_a, w_br_b, w_br_c, w_out):
    g_a, g_b, g_c = jnp.split(jax.nn.sigmoid(cols(z, COL_GATE, N_BRANCH * D_MODEL)), N_BRANCH, axis=-1)
    merged = g_a * (a @ w_br_a) + g_b * (b @ w_br_b) + g_c * (c @ w_br_c)
    return merged @ w_out


def token_mix(h, hc, need_ctx, w_in, a_v_gain, a_ws, a_bs, b_dw1, b_dw2, b_db, b_norm_g,
              c_q_gain, c_k_gain, c_sink, w_br_a, w_br_b, w_br_c, w_out):
    B_ = h.shape[0]
    z = h @ w_in
    zc = hc @ (w_in if need_ctx else w_in[:, :CTX_STATE_COLS])

    a_out = chunk_spatial_gating(cols(z, COL_AU, A_WIDTH), cols(z, COL_AV, A_WIDTH), a_v_gain, a_ws, a_bs)

    def gla_kvg(zz, hh):
        k = heads(zz, COL_BK, B_KW, B_HEADS, B_DK).astype(jnp.float32)
        v = heads(zz, COL_BV, B_VW, B_HEADS, B_DV).astype(jnp.float32)
        g_f = gla_log_decay(hh, b_dw1[0], b_dw2[0], b_db[0])
        g_b = gla_log_decay(hh, b_dw1[1], b_dw2[1], b_db[1])
        return k, v, g_f, g_b

    k, v, g_f, g_b = gla_kvg(z, h)
    q = heads(z, COL_BQ, B_KW, B_HEADS, B_DK).astype(jnp.float32) * B_DK ** -0.5
    kc, vc, gc_f, gc_b = gla_kvg(zc, hc)
    if need_ctx:
        qc = heads(zc, COL_BQ, B_KW, B_HEADS, B_DK).astype(jnp.float32) * B_DK ** -0.5
        zero = jnp.zeros((B_, B_HEADS, B_DK, B_DV), jnp.float32)
        oc_f, s_f = gla_chunked(qc, kc, vc, gc_f, zero)
        oc_b, s_b = gla_chunked(flip(qc), flip(kc), flip(vc), flip(gc_b), zero)
    else:
        s_f = gla_final_state(kc, vc, gc_f)
        s_b = gla_final_state(flip(kc), flip(vc), flip(gc_b))
    o_f, _ = gla_chunked(q, k, v, g_f, s_f)
    o_b, _ = gla_chunked(flip(q), flip(k), flip(v), flip(g_b), s_b)
    b_out = gla_output(o_f + flip(o_b), cols(z, COL_BG, B_VW), b_norm_g)

    def kv_heads(zz):
        kk = rms_norm(heads(zz, COL_CK, C_KVW, C_KV_HEADS, HEAD_DIM), c_k_gain)
        vv = heads(zz, COL_CV, C_KVW, C_KV_HEADS, HEAD_DIM)
        return kk, vv

    k_att, v_att = kv_heads(z)
    kc_att, vc_att = kv_heads(zc)
    q_att = rope_2d(rms_norm(heads(z, COL_CQ, C_QW, C_HEADS, HEAD_DIM), c_q_gain))
    c_out = window_attention(q_att, rope_2d(k_att), v_att, kc_att, vc_att, c_sink)

    out = branch_merge(z, a_out, b_out, c_out, w_br_a, w_br_b, w_br_c, w_out)
    if not need_ctx:
        return out, None
    ac_out = chunk_spatial_gating(cols(zc, COL_AU, A_WIDTH), cols(zc, COL_AV, A_WIDTH), a_v_gain, a_ws, a_bs)
    bc_out = gla_output(oc_f + flip(oc_b), cols(zc, COL_BG, B_VW), b_norm_g)
    qc_att = rms_norm(heads(zc, COL_CQ, C_QW, C_HEADS, HEAD_DIM), c_q_gain)
    cc_out = context_attention(qc_att, kc_att, vc_att, c_sink)
    out_c = branch_merge(zc, ac_out, bc_out, cc_out, w_br_a, w_br_b, w_br_c, w_out)
    return out, out_c


def setup_inputs(seed: int = 0) -> dict:
    key = jax.random.key(seed)
    ks = jax.random.split(key, 24)
    D = D_MODEL

    def nrm(k, shape, scale):
        return jax.random.normal(k, shape, jnp.float32) * scale

    return {
        'x': nrm(ks[0], (BATCH, SEQ, D), 1.0),
        'c': nrm(ks[1], (BATCH, D), 1.0),
        'ctx': nrm(ks[2], (BATCH, CTX_LEN, D), 1.0),
        'c_ctx': nrm(ks[3], (D,), 1.0),
        'w_ada': nrm(ks[4], (DEPTH, D, N_MOD * D), 0.5 * D ** -0.5),
        'b_ada': nrm(ks[5], (DEPTH, N_MOD * D), 0.02),
        'norm_g': 1.0 + nrm(ks[6], (DEPTH, 3, D), 0.05),
        'w_ffn_up': nrm(ks[7], (DEPTH, 2, D, 2 * FFN_DIM), D ** -0.5),
        'w_ffn_down': nrm(ks[8], (DEPTH, 2, FFN_DIM, D), FFN_DIM ** -0.5),
        'w_in': nrm(ks[9], (DEPTH, D, IN_COLS), D ** -0.5),
        'a_v_gain': 1.0 + nrm(ks[10], (DEPTH, A_WIDTH), 0.05),
        'a_ws': nrm(ks[11], (DEPTH, A_GROUPS, A_CHUNK, A_CHUNK), A_CHUNK ** -0.5),
        'a_bs': 1.0 + nrm(ks[12], (DEPTH, A_GROUPS, A_CHUNK), 0.05),
        'b_decay_w1': nrm(ks[13], (DEPTH, 2, D, B_RANK), D ** -0.5),
        'b_decay_w2': nrm(ks[14], (DEPTH, 2, B_RANK, B_KW), B_RANK ** -0.5),
        'b_decay_b': nrm(ks[15], (DEPTH, 2, B_KW), 0.1),
        'b_norm_g': 1.0 + nrm(ks[16], (DEPTH, B_DV), 0.05),
        'c_q_gain': 1.0 + nrm(ks[17], (DEPTH, HEAD_DIM), 0.05),
        'c_k_gain': 1.0 + nrm(ks[18], (DEPTH, HEAD_DIM), 0.05),
        'c_sink': nrm(ks[19], (DEPTH, C_HEADS), 0.5),
        'w_br_a': nrm(ks[20], (DEPTH, A_WIDTH, D), A_WIDTH ** -0.5),
        'w_br_b': nrm(ks[21], (DEPTH, B_VW, D), B_VW ** -0.5),
        'w_br_c': nrm(ks[22], (DEPTH, C_QW, D), C_QW ** -0.5),
        'w_out': nrm(ks[23], (DEPTH, D, D), D ** -0.5),
    }


def reference(x, c, ctx, c_ctx, w_ada, b_ada, norm_g, w_ffn_up, w_ffn_down, w_in, a_v_gain, a_ws, a_bs,
              b_decay_w1, b_decay_w2, b_decay_b, b_norm_g, c_q_gain, c_k_gain, c_sink,
              w_br_a, w_br_b, w_br_c, w_out):
    D = D_MODEL
    c_act = jax.nn.silu(c)
    cc_act = jax.nn.silu(c_ctx)
    xc = ctx
    for l in range(DEPTH):
        last = l == DEPTH - 1
        m = jnp.split((c_act @ w_ada[l] + b_ada[l])[:, None, :], N_MOD, axis=-1)
        n_c = 5 if last else N_MOD
        mc = jnp.split(cc_act @ w_ada[l][:, :n_c * D] + b_ada[l][:n_c * D], n_c)
        x = x + 0.5 * m[2] * swiglu(modulate(rms_norm(x, norm_g[l, 0]), m[0], m[1]), w_ffn_up[l, 0], w_ffn_down[l, 0])
        xc = xc + 0.5 * mc[2] * swiglu(modulate(rms_norm(xc, norm_g[l, 0]), mc[0], mc[1]), w_ffn_up[l, 0], w_ffn_down[l, 0])
        h = modulate(rms_norm(x, norm_g[l, 1]), m[3], m[4])
        hc = modulate(rms_norm(xc, norm_g[l, 1]), mc[3], mc[4])
        mix, mix_c = token_mix(h, hc, not last, w_in[l], a_v_gain[l], a_ws[l], a_bs[l],
                               b_decay_w1[l], b_decay_w2[l], b_decay_b[l], b_norm_g[l],
                               c_q_gain[l], c_k_gain[l], c_sink[l],
                               w_br_a[l], w_br_b[l], w_br_c[l], w_out[l])
        x = x + m[5] * mix
        x = x + 0.5 * m[8] * swiglu(modulate(rms_norm(x, norm_g[l, 2]), m[6], m[7]), w_ffn_up[l, 1], w_ffn_down[l, 1])
        if not last:
            xc = xc + mc[5] * mix_c
            xc = xc + 0.5 * mc[8] * swiglu(modulate(rms_norm(xc, norm_g[l, 2]), mc[6], mc[7]), w_ffn_up[l, 1], w_ffn_down[l, 1])
    return x
```

```python
import numpy as np
from contextlib import ExitStack
import concourse.bass as bass
import concourse.mybir as mybir
from concourse.bass_utils import run_bass_kernel_spmd

F32 = mybir.dt.float32
F32R = mybir.dt.float32r
AF = mybir.ActivationFunctionType
ALU = mybir.AluOpType

D = 2048; KT = 16; TC = 256; FF = 5632; FT = 44; NL = 2
MODE = 'full'


def set_mode(mode):
    global MODE, TL, NTOK, NT, TILES
    MODE = mode
    TL = 4096 if mode == 'full' else 2048
    NTOK = TL + TC
    NT = TL // 128
    TILES = [(i * 512, 512) for i in range(TL // 512)] + [(TL, 256)]


NCORE = 8
EPS = 1e-6
set_mode(MODE)
FM = {'bk': (0, 2), 'bq': (2, 2), 'au': (4, 4), 'bg': (8, 4), 'ck': (12, 2), 'cq': (14, 8), 'gate': (22, 48)}
COLS = {'bk': 0, 'bv': 256, 'ck': 768, 'cv': 1024, 'au': 1280, 'av': 1792, 'bq': 2304, 'bg': 2560, 'cq': 3072, 'gate': 4096}
NQ = 12
DEBUG_STOP = None
GLA_DBG = 9


class Tr:
    def __init__(self, nc, es):
        self.nc = nc
        self.E = {'pe': nc.tensor, 'act': nc.scalar, 'dve': nc.vector, 'pool': nc.gpsimd, 'sp': nc.sync}
        self.sem = {}
        self.val = {}
        for e in self.E:
            self.sem[e] = es.enter_context(nc.semaphore('s_' + e)); self.val[e] = 0
        self.dq = {}
        for q in ('sp', 'pool', 'act'):
            ids = []
            for i in range(NQ):
                sid = f'd_{q}{i}'
                self.sem[sid] = es.enter_context(nc.semaphore(sid)); self.val[sid] = 0
                ids.append(sid)
            self.dq[q] = [ids, 0]
        self.seen = {e: {} for e in self.E}
        self.w = {}
        self.r = {}

    def _wait(self, e, sid, v):
        if v <= 0 or self.seen[e].get(sid, 0) >= v:
            return
        self.E[e].wait_ge(self.sem[sid], v)
        self.seen[e][sid] = v

    def _sync(self, e, r, w):
        for k in r:
            ev = self.w.get(k)
            if ev is not None and not (ev[0] == e == 'pe'):
                self._wait(e, ev[0], ev[1])
        for k in w:
            ev = self.w.get(k)
            if ev is not None and not (ev[0] == e == 'pe'):
                self._wait(e, ev[0], ev[1])
            for sid, v in self.r.get(k, {}).items():
                if not (sid == e == 'pe'):
                    self._wait(e, sid, v)

    def _record(self, ev, r, w):
        for k in r:
            d = self.r.setdefault(k, {})
            d[ev[0]] = max(d.get(ev[0], 0), ev[1])
        for k in w:
            self.w[k] = ev
            self.r[k] = {}

    def op(self, e, fn, r=(), w=()):
        self._sync(e, r, w)
        inst = fn(self.E[e])
        self.val[e] += 1
        inst.then_inc(self.sem[e], 1)
        self._record((e, self.val[e]), r, w)

    def dma(self, q, out, in_, r=(), w=()):
        ids, nxt = self.dq[q]
        sid = ids[nxt]; self.dq[q][1] = (nxt + 1) % NQ
        self._wait(q, sid, self.val[sid])
        self._sync(q, r, w)
        inst = self.E[q].dma_start(out=out, in_=in_)
        self.val[sid] += 16
        inst.then_inc(self.sem[sid], 16)
        self._record((sid, self.val[sid]), r, w)

    def barrier(self):
        for e in self.E:
            for sid, v in self.val.items():
                if sid == e == 'pe':
                    continue
                self._wait(e, sid, v)
        self.w.clear(); self.r.clear()


class Ring:
    def __init__(self, kb, st, name, shape, dt, n):
        kb.uid += 1
        self.t = [st.enter_context(kb.nc.sbuf_tensor(f'{name}{i}_{kb.uid}', shape, dt)) for i in range(n)]
        self.name = name; self.i = 0

    def next(self):
        i = self.i; self.i = (i + 1) % len(self.t)
        return self.t[i], (self.name, i)


class KB:
    def __init__(self, launch):
        self.launch = launch
        self.nc = nc = bass.Bass("TRN2", target_bir_lowering=False)
        self.es = ExitStack()
        self.T = Tr(nc, self.es)
        self.din = {}
        self.psb = [self.es.enter_context(nc.psum_tensor(f'ps{i}', [128, 512], F32)) for i in range(8)]
        self.psi = 0
        self.uid = 0

    def inp(self, name, shape, dt=F32):
        self.din[name] = self.nc.dram_tensor(name, list(shape), dt, kind="ExternalInput").ap()
        return self.din[name]

    def outp(self, name, shape, dt=F32):
        return self.nc.dram_tensor(name, list(shape), dt, kind="ExternalOutput").ap()

    def scr(self, name, shape, dt=F32):
        return self.nc.dram_tensor(name, list(shape), dt, kind="Internal").ap()

    def sb(self, st, name, shape, dt=F32):
        self.uid += 1
        return st.enter_context(self.nc.sbuf_tensor(f'{name}_{self.uid}', list(shape), dt))

    def ps(self):
        i = self.psi; self.psi = (i + 1) % 8
        return self.psb[i], ('ps', i)

    def mm(self, out, lhsT, rhs, start, stop, r, w):
        self.T.op('pe', lambda e: e.matmul(out, lhsT, rhs, start=start, stop=stop), r, w)

    def act(self, out, in_, func, r, w, bias=None, scale=None):
        kw = {}
        if bias is not None:
            kw['bias'] = bias
        if scale is not None:
            kw['scale'] = scale
        self.T.op('act', lambda e: e.activation(out=out, in_=in_, func=func, **kw), r, w)

    def tt(self, out, in0, in1, op, r, w, eng='dve'):
        self.T.op(eng, lambda e: e.tensor_tensor(out=out, in0=in0, in1=in1, op=op), r, w)

    def ts(self, out, in0, s1, s2, op0, op1, r, w, eng='dve'):
        if op1 is None:
            self.T.op(eng, lambda e: e.tensor_scalar(out=out, in0=in0, scalar1=s1, scalar2=None, op0=op0), r, w)
        else:
            self.T.op(eng, lambda e: e.tensor_scalar(out=out, in0=in0, scalar1=s1, scalar2=s2, op0=op0, op1=op1), r, w)

    def stt(self, out, in0, scalar, in1, op0, op1, r, w):
        self.T.op('dve', lambda e: e.scalar_tensor_tensor(out=out, in0=in0, scalar=scalar, in1=in1, op0=op0, op1=op1), r, w)

    def ld(self, out, in_, r, w, q=None, force=None):
        if out.dtype == F32R and in_.dtype == F32R and str(out.space).endswith('DRAM'):
            out = out.bitcast(F32); in_ = in_.bitcast(F32)
        q = 'pool' if (out.dtype == F32R or in_.dtype == F32R) else 'sp'
        if force is not None and q == 'sp':
            q = force
        shp = tuple(out.shape)
        if len(shp) == 3 and shp[1] > 4 and tuple(in_.shape) == shp:
            for a in range(0, shp[1], 4):
                b = min(shp[1], a + 4)
                self.T.dma(q, out[:, a:b, :], in_[:, a:b, :], r, w)
            return
        self.T.dma(q, out, in_, r, w)

    LAZY = {'wada': ([144, 128, KT * 128], 1), 'wup': ([FT, 128, 2, KT * 128], 2), 'wdn': ([KT, 4, 128, 11 * 128], 2),
            'winF': ([70, 128, KT * 128], 1), 'winT': ([128, KT, 1280], 1), 'wdec1': ([128, KT * 128], 1),
            'wdec2': ([128, 2, 256], 1), 'awsT': ([128, 4, 128], 1), 'wbr': ([KT, 128, KT * 128], 1), 'wout': ([KT, 128, KT * 128], 1)}

    def Wt(self, name, *idx):
        key = name + ''.join(str(i) for i in idx)
        if key not in self._lazy:
            self._lazy[key] = self.inp(key, self.LAZY[name][0], F32R)
        return self._lazy[key]

    def setup(self):
        nc = self.nc; es = self.es
        L = self.launch
        self.x_in = self.inp('xT', [KT, 128, NTOK]) if L in (0, 'full') else self.inp('xs_in', [KT, 128, NTOK])
        self.cT = self.inp('cT', [128, KT, 2])
        self.consts = self.inp('consts', [128, 6, 128])
        self.masks = self.inp('masks', [128, 5, 512])
        self.rope = self.inp('rope', [128, 2, NTOK])
        self._lazy = {}
        self.normg = self.inp('normg', [128, NL, 3, KT])
        self.badaT = self.inp('badaT', [128, NL, 144])
        self.bdec = self.inp('bdec', [128, NL, 2, 2])
        self.avg = self.inp('avg', [128, NL, 512])
        self.absb = self.inp('absb', [128, NL, 512])
        self.smallg = self.inp('smallg', [128, NL, 4])
        self.sinkb = self.inp('sinkb', [128, NL, 8])
        if L in (1, 2):
            self.mod_in = self.inp('mod_in', [128, NL * 144 * 2])
            self.partner = self.inp('partner', [128, 768])
        self.xs = self.outp('xs', [KT, 128, NTOK]) if L in (0, 1) else self.scr('xs', [KT, 128, NTOK])
        if L in (2, 'full'):
            self.y = self.outp('yT', [KT, 128, TL])
        if L in (0, 1):
            self.export = self.outp('export', [128, 768])
        if L == 0:
            self.mod_out = self.outp('mod_out', [128, NL * 144 * 2])
        S = self.scr
        self.kT = S('kT', [2, 128, NTOK]); self.qT = S('qT', [2, 128, NTOK]); self.gT = S('gT', [2, 2, 128, NTOK])
        self.gv = S('gv', [NTOK, 512]); self.ogT = S('ogT', [4, 128, NTOK])
        self.QT = S('QT', [8, 128, NTOK], F32R); self.KTs = S('KTs', [2, 128, NTOK], F32R); self.Vs = S('Vs', [NTOK, 256], F32R)
        self.uT = S('uT', [4, 128, NTOK]); self.av = S('av', [NTOK, 512], F32R)
        self.mixT = S('mixT', [KT, 128, NTOK], F32R)
        g = lambda n, s, d=F32: es.enter_context(nc.sbuf_tensor(n, s, d))
        self.cst = g('cst', [128, 6, 128])
        self.cstr = g('cstr', [128, 3, 128], F32R)
        self.modraw = g('modraw', [128, NL, 144, 2])
        self.nA = g('nA', [128, NL, 3, 2, KT]); self.nB = g('nB', [128, NL, 3, 2, KT]); self.nG = g('nG', [128, NL, 3, 2, KT])
        self.ngt = g('ngt', [128, NL, 3, KT])
        self.smg = g('smg', [128, NL, 4])
        self.ld(self.cst[:], self.consts[:, :, :], [], ['cst'])
        self.ld(self.ngt[:], self.normg[:, :, :, :], [], ['ngt'])
        self.ld(self.smg[:], self.smallg[:, :, :], [], ['smg'])
        self.T.op('dve', lambda e: e.tensor_copy(out=self.cstr[:], in_=self.cst[:, 0:3, :]), ['cst'], ['cstr'])
        self.ones = self.cstr[:, 0, :]; self.ident = self.cst[:, 1, :]; self.ropeP = self.cstr[:, 2, :]
        self.rmask = self.cst[:, 3, :]

    def mods(self):
        T = self.T
        with ExitStack() as st:
            if self.launch in (0, 'full'):
                craw = self.sb(st, 'craw', [128, KT, 2]); cact = self.sb(st, 'cact', [128, KT, 2], F32R)
                bad = self.sb(st, 'bad', [128, NL, 144])
                self.ld(craw[:], self.cT[:, :, :], [], ['craw'])
                self.ld(bad[:], self.badaT[:, :, :], [], ['bad'])
                self.act(cact[:], craw[:], AF.Silu, ['craw'], ['cact'])
                wr = Ring(self, st, 'wad', [128, KT * 128], F32R, 4)
                for l in range(NL):
                    pt, pk = self.ps()
                    for j in range(144):
                        wt, wk = wr.next()
                        self.ld(wt[:], self.Wt('wada', l)[j, :, :], [], [wk])
                        for kt in range(KT):
                            self.mm(pt[:, 2 * j:2 * j + 2], wt[:, kt * 128:(kt + 1) * 128], cact[:, kt, :],
                                    kt == 0, kt == KT - 1, [wk, 'cact'], [pk])
                    for wch in range(2):
                        self.tt(self.modraw[:, l, :, wch], pt[:, wch:288:2], bad[:, l, :], ALU.add, [pk, 'bad'], [('modraw', l, wch)])
                T.barrier()
                if self.launch == 0:
                    self.ld(self.mod_out[:, :], self.modraw[:].rearrange("p l j w -> p (l j w)"), [], ['mod_out'], q='pool')
            else:
                self.ld(self.modraw[:].rearrange("p l j w -> p (l j w)"), self.mod_in[:, :], [], [('modraw', l, w) for l in range(NL) for w in range(2)])
            for l in range(NL):
                for i in range(3):
                    for wch in range(2):
                        rk = [('modraw', l, wch), 'ngt']
                        sc = self.modraw[:, l, (3 * i + 1) * KT:(3 * i + 2) * KT, wch]
                        self.stt(self.nA[:, l, i, wch, :], sc, 1.0, self.ngt[:, l, i, :], ALU.add, ALU.mult, rk, ['nA'])
                        self.T.op('dve', lambda e: e.tensor_copy(out=self.nB[:, l, i, wch, :], in_=self.modraw[:, l, (3 * i) * KT:(3 * i + 1) * KT, wch]), rk, ['nB'])
                        self.ts(self.nG[:, l, i, wch, :], self.modraw[:, l, (3 * i + 2) * KT:(3 * i + 3) * KT, wch],
                                (1.0 if i == 1 else 0.5), None, ALU.mult, None, rk, ['nG'])
            T.barrier()

    def norm_mod(self, R, src, l, i, wch, tok0, n, hT, hoff=0, hk='hT'):
        xs_r, sq_r, tmp_r = R['xs'], R['sq'], R['tmp']
        rstd = R['rstd']
        pt, pk = self.ps()
        for kt in range(KT):
            xt, xk = xs_r.next()
            self.ld(xt[:, :n], src[kt, :, tok0:tok0 + n], [('X', kt, tok0)], [xk], q='pool')
            sq, sk = sq_r.next()
            self.act(sq[:, :n], xt[:, :n], AF.Square, [xk], [sk])
            self.mm(pt[:, :n], self.ones, sq[:, :n], kt == 0, kt == KT - 1, [sk, 'cstr'], [pk])
        self.act(rstd[:, :n], pt[:, :n], AF.Sqrt, [pk], ['rstd'], bias=self.epsD[:, 0:1], scale=1.0 / D)
        self.T.op('dve', lambda e: e.reciprocal(out=rstd[:, :n], in_=rstd[:, :n]), ['rstd'], ['rstd'])
        for kt in range(KT):
            xt, xk = xs_r.next()
            self.ld(xt[:, :n], src[kt, :, tok0:tok0 + n], [('X', kt, tok0)], [xk], q='pool')
            tm, tk = tmp_r.next()
            self.tt(tm[:, :n], xt[:, :n], rstd[:, :n], ALU.mult, [xk, 'rstd'], [tk])
            self.act(hT[:, kt, hoff:hoff + n], tm[:, :n], AF.Identity, [tk, 'nA', 'nB'], [(hk, kt, hoff)],
                     bias=self.nB[:, l, i, wch, kt:kt + 1], scale=self.nA[:, l, i, wch, kt:kt + 1])

    def eps_tile(self, st):
        self.epsD = self.sb(st, 'epsD', [128, 2])
        self.T.op('dve', lambda e: e.memset(self.epsD[:], EPS), [], ['epsD'])

    def ffn(self, l, f, src, dst, tiles, dst_tok_off=0):
        i = 0 if f == 0 else 2
        QH = FT // 4
        ftiles = []
        for (t0, n) in tiles:
            if n == 512 and ftiles and ftiles[-1][1] == 512 and ftiles[-1][0] + 512 == t0:
                ftiles[-1] = (ftiles[-1][0], 1024)
            else:
                ftiles.append((t0, n))
        with ExitStack() as st:
            self.eps_tile(st)
            aT = self.sb(st, 'aT', [128, QH, 1024], F32R)
            hT = self.sb(st, 'hT', [128, KT, 1024], F32R)
            R = {'xs': Ring(self, st, 'xs', [128, 512], F32, 3), 'sq': Ring(self, st, 'sq', [128, 512], F32R, 2),
                 'tmp': Ring(self, st, 'tmp', [128, 512], F32, 2), 'rstd': self.sb(st, 'rstd', [128, 512])}
            wu = Ring(self, st, 'wu', [128, 2, KT * 128], F32R, 2)
            wd = Ring(self, st, 'wd', [128, QH * 128], F32R, 3)
            sg_r = Ring(self, st, 'sg', [128, 512], F32, 2)
            xr_r = Ring(self, st, 'xr', [128, 512], F32, 3)
            xo_r = Ring(self, st, 'xo', [128, 512], F32, 3)

            def halves(n):
                return [(0, min(n, 512))] + ([(512, n - 512)] if n > 512 else [])

            def norm(ti):
                tok0, n = ftiles[ti]
                for (h0, hn) in halves(n):
                    self.norm_mod(R, src, l, i, 1 if tok0 >= TL else 0, tok0 + h0, hn, hT, hoff=h0)

            def up(ti, q):
                tok0, n = ftiles[ti]
                for jj in range(QH):
                    j = q * QH + jj
                    wt, wk = wu.next()
                    self.ld(wt[:], self.Wt('wup', l, f)[j, :, :, :], [], [wk])
                    for (h0, hn) in halves(n):
                        pg, pgk = self.ps(); pv, pvk = self.ps()
                        for kt in range(KT):
                            self.mm(pg[:, :hn], wt[:, 0, kt * 128:(kt + 1) * 128], hT[:, kt, h0:h0 + hn], kt == 0, kt == KT - 1, [wk, ('hT', kt, h0)], [pgk])
                        for kt in range(KT):
                            self.mm(pv[:, :hn], wt[:, 1, kt * 128:(kt + 1) * 128], hT[:, kt, h0:h0 + hn], kt == 0, kt == KT - 1, [wk, ('hT', kt, h0)], [pvk])
                        sg, sk = sg_r.next()
                        self.act(sg[:, :hn], pg[:, :hn], AF.Silu, [pgk], [sk])
                        self.tt(aT[:, jj, h0:h0 + hn], sg[:, :hn], pv[:, :hn], ALU.mult, [sk, pvk], [('aT', jj, h0)])

            def down(ti, q, nts):
                tok0, n = ftiles[ti]; wch = 1 if tok0 >= TL else 0
                for nt in nts:
                    wt, wk = wd.next()
                    self.ld(wt[:], self.Wt('wdn', l, f)[nt, q, :, :], [], [wk])
                    for (h0, hn) in halves(n):
                        pt, pk = self.ps()
                        for kk in range(QH):
                            self.mm(pt[:, :hn], wt[:, kk * 128:(kk + 1) * 128], aT[:, kk, h0:h0 + hn], kk == 0, kk == QH - 1, [wk, ('aT', kk, h0)], [pk])
                        xt, xk = xr_r.next()
                        rsrc = src if q == 0 else dst
                        roff = 0 if q == 0 else dst_tok_off
                        t_in = tok0 + h0 - roff; t_out = tok0 + h0 - dst_tok_off
                        self.ld(xt[:, :hn], rsrc[nt, :, t_in:t_in + hn], [('X', q == 0 and src is not dst, nt, tok0 + h0)], [xk])
                        xo, ok = xo_r.next()
                        self.stt(xo[:, :hn], pt[:, :hn], self.nG[:, l, i, wch, nt:nt + 1], xt[:, :hn], ALU.mult, ALU.add, [pk, xk, 'nG'], [ok])
                        self.ld(dst[nt, :, t_out:t_out + hn], xo[:, :hn], [ok], [('X', False, nt, tok0 + h0)], force='act')

            norm(0)
            for ti in range(len(ftiles)):
                for q in range(4):
                    up(ti, q)
                    if q == 3 and ti + 1 < len(ftiles):
                        down(ti, q, range(0, 8))
                        norm(ti + 1)
                        down(ti, q, range(8, 16))
                    else:
                        down(ti, q, range(KT))
            self.T.barrier()

    def inproj(self, l, tiles, names):
        with ExitStack() as st:
            self.eps_tile(st)
            hTs = [self.sb(st, f'hT{i_}', [128, KT, 512], F32R) for i_ in range(2)]
            R = {'xs': Ring(self, st, 'xs', [128, 512], F32, 3), 'sq': Ring(self, st, 'sq', [128, 512], F32R, 2),
                 'tmp': Ring(self, st, 'tmp', [128, 512], F32, 2), 'rstd': self.sb(st, 'rstd', [128, 512])}
            wf = Ring(self, st, 'wf', [128, KT * 128], F32R, 3)
            wtm = Ring(self, st, 'wtm', [128, KT // 2, 512], F32R, 3)
            so = Ring(self, st, 'so', [128, 512], F32, 6)
            sr = Ring(self, st, 'sr', [128, 512], F32R, 6)
            w2 = self.sb(st, 'w2', [128, 2, 256], F32R); bd = self.sb(st, 'bd', [128, 2, 2])
            avg = self.sb(st, 'avg', [128, 512]); rp = self.sb(st, 'rp', [128, 2, 512])
            bst = self.sb(st, 'bst', [128, 8]); mv = self.sb(st, 'mv', [128, 4])
            self.ld(w2[:], self.Wt('wdec2', l)[:, :, :], [], ['w2']); self.ld(bd[:], self.bdec[:, l, :, :], [], ['bd'])
            self.ld(avg[:], self.avg[:, l, :], [], ['avg'])

            def fm(ti, widx):
                tok0, n = tiles[ti]
                hT = hTs[ti % 2]; hk = f'hT{ti % 2}'
                wt, wk = wf.next()
                self.ld(wt[:], widx, [], [wk])
                pt, pk = self.ps()
                for kt in range(KT):
                    self.mm(pt[:, :n], wt[:, kt * 128:(kt + 1) * 128], hT[:, kt, :n], kt == 0, kt == KT - 1, [wk, (hk, kt, 0)], [pk])
                return pt, pk

            def norm(ti):
                tok0, n = tiles[ti]
                self.norm_mod(R, self.xs, l, 1, 1 if tok0 >= TL else 0, tok0, n, hTs[ti % 2], hk=f'hT{ti % 2}')

            norm(0)
            for ti in range(len(tiles)):
                tok0, n = tiles[ti]; wch = 1 if tok0 >= TL else 0
                hT = hTs[ti % 2]; hk = f'hT{ti % 2}'
                if 'ck' in names:
                    self.ld(rp[:, :, :n], self.rope[:, :, tok0:tok0 + n], [('rp',)], ['rp'], q='pool')
                for nm in ('bk', 'bq', 'au', 'bg'):
                    if nm not in names:
                        continue
                    t0, cnt = FM[nm]
                    for j in range(cnt):
                        pt, pk = fm(ti, self.Wt('winF', l)[t0 + j, :, :])
                        o, ok = so.next()
                        if nm == 'bk':
                            self.act(o[:, :n], pt[:, :n], AF.Copy, [pk], [ok])
                            dstap = self.kT[j, :, tok0:tok0 + n]
                        elif nm == 'bq':
                            self.act(o[:, :n], pt[:, :n], AF.Copy, [pk], [ok], scale=0.125)
                            dstap = self.qT[j, :, tok0:tok0 + n]
                        elif nm == 'au':
                            self.act(o[:, :n], pt[:, :n], AF.Gelu_apprx_tanh, [pk], [ok])
                            dstap = self.uT[j, :, tok0:tok0 + n]
                        else:
                            self.act(o[:, :n], pt[:, :n], AF.Silu, [pk], [ok])
                            dstap = self.ogT[j, :, tok0:tok0 + n]
                        self.ld(dstap, o[:, :n], [ok], [(nm, j, tok0)], q='pool')
                for nm in ('ck', 'cq'):
                    if nm not in names:
                        continue
                    t0, cnt = FM[nm]
                    gcol = self.smg[:, l, 2:3] if nm == 'ck' else self.smg[:, l, 1:2]
                    for j in range(cnt):
                        pt, pk = fm(ti, self.Wt('winF', l)[t0 + j, :, :])
                        qg, qk = sr.next(); sq, sk = sr.next()
                        self.act(qg[:, :n], pt[:, :n], AF.Identity, [pk, 'smg'], [qk], scale=gcol)
                        self.act(sq[:, :n], pt[:, :n], AF.Square, [pk], [sk])
                        p2, p2k = self.ps(); p3, p3k = self.ps()
                        self.mm(p2[:, :n], self.ones, sq[:, :n], True, True, [sk, 'cstr'], [p2k])
                        self.mm(p3[:, :n], self.ropeP, qg[:, :n], True, True, [qk, 'cstr'], [p3k])
                        rs, rk = so.next()
                        self.act(rs[:, :n], p2[:, :n], AF.Sqrt, [p2k], [rk], bias=self.epsD[:, 0:1], scale=1.0 / 128)
                        self.T.op('dve', lambda e: e.reciprocal(out=rs[:, :n], in_=rs[:, :n]), [rk], [rk])
                        t1, t1k = so.next(); t2, t2k = so.next()
                        self.tt(t1[:, :n], qg[:, :n], rp[:, 0, :n], ALU.mult, [qk, 'rp'], [t1k])
                        self.tt(t2[:, :n], p3[:, :n], rp[:, 1, :n], ALU.mult, [p3k, 'rp'], [t2k])
                        self.tt(t1[:, :n], t1[:, :n], t2[:, :n], ALU.add, [t1k, t2k], [t1k])
                        o, ok = sr.next()
                        self.tt(o[:, :n], t1[:, :n], rs[:, :n], ALU.mult, [t1k, rk], [ok])
                        dstap = (self.KTs if nm == 'ck' else self.QT)[j, :, tok0:tok0 + n]
                        self.ld(dstap, o[:, :n], [ok], [(nm, j, tok0)], q='pool')
                if 'dec' in names:
                    pt, pk = fm(ti, self.Wt('wdec1', l)[:, :])
                    rT, rTk = sr.next()
                    self.act(rT[:, :n], pt[:, :n], AF.Copy, [pk], [rTk])
                    for d_ in range(2):
                        for hp in range(2):
                            p2, p2k = self.ps()
                            self.mm(p2[:, :n], w2[:, d_, hp * 128:(hp + 1) * 128], rT[:, :n], True, True, ['w2', rTk], [p2k])
                            o, ok = so.next()
                            self.act(o[:, :n], p2[:, :n], AF.Sigmoid, [p2k, 'bd'], [ok], bias=bd[:, d_, hp:hp + 1])
                            self.act(o[:, :n], o[:, :n], AF.Ln, [ok], [ok])
                            o2, o2k = so.next()
                            self.ts(o2[:, :n], o[:, :n], 1.0 / 16.0, None, ALU.mult, None, [ok], [o2k])
                            self.ld(self.gT[d_, hp, :, tok0:tok0 + n], o2[:, :n], [o2k], [('gT', d_, hp, tok0)], q='pool')
                if ti + 1 < len(tiles):
                    norm(ti + 1)
                for nm, c0, nc_, dst in (('bv', 0, 512, self.gv), ('cv', 512, 256, self.Vs), ('av', 768, 512, self.av)):
                    if nm not in names:
                        continue
                    wth = [wtm.next() for _ in range(2)]
                    for hf in range(2):
                        self.ld(wth[hf][0][:, :, :nc_], self.Wt('winT', l)[:, hf * 8:(hf + 1) * 8, c0:c0 + nc_], [], [wth[hf][1]])
                    for sub in range(n // 128):
                        pt, pk = self.ps()
                        for kt in range(KT):
                            wt, wk = wth[kt // 8]
                            self.mm(pt[:, :nc_], hT[:, kt, sub * 128:(sub + 1) * 128], wt[:, kt % 8, :nc_], kt == 0, kt == KT - 1, [wk, (hk, kt, 0)], [pk])
                        r0 = tok0 + sub * 128
                        if nm == 'bv':
                            o, ok = so.next()
                            self.act(o[:, :nc_], pt[:, :nc_], AF.Copy, [pk], [ok])
                        elif nm == 'cv':
                            o, ok = sr.next()
                            self.act(o[:, :nc_], pt[:, :nc_], AF.Copy, [pk], [ok])
                        else:
                            g_, gk = so.next()
                            self.act(g_[:, :], pt[:, :], AF.Gelu_apprx_tanh, [pk], [gk])
                            self.T.op('dve', lambda e: e.bn_stats(out=bst[:, 0:6], in_=g_[:, :]), [gk], ['bst'])
                            self.T.op('dve', lambda e: e.bn_aggr(out=mv[:, 0:2], in_=bst[:, 0:6]), ['bst'], ['mv'])
                            self.act(mv[:, 2:3], mv[:, 1:2], AF.Sqrt, ['mv'], ['mv'], bias=self.epsD[:, 0:1], scale=1.0)
                            self.T.op('dve', lambda e: e.reciprocal(out=mv[:, 3:4], in_=mv[:, 2:3]), ['mv'], ['mv'])
                            self.ts(g_[:, :], g_[:, :], mv[:, 0:1], mv[:, 3:4], ALU.subtract, ALU.mult, [gk, 'mv'], [gk])
                            o, ok = sr.next()
                            self.tt(o[:, :], g_[:, :], avg[:, :], ALU.mult, [gk, 'avg'], [ok])
                        self.ld(dst[r0:r0 + 128, :], o[:, :nc_], [ok], [(nm, r0)], q='pool')
            self.T.barrier()

    def mixa(self, l, ntiles):
        with ExitStack() as st:
            ws = self.sb(st, 'ws', [128, 4, 128], F32R); bsb = self.sb(st, 'bsb', [128, 512])
            self.ld(ws[:], self.Wt('awsT', l)[:, :, :], [], ['ws']); self.ld(bsb[:], self.absb[:, l, :], [], ['bsb'])
            vr = Ring(self, st, 'vr', [128, 512], F32R, 2); ur = Ring(self, st, 'ur', [128, 4, 128], F32, 2)
            tr_ = Ring(self, st, 'tr', [128, 512], F32, 2); orr = Ring(self, st, 'or', [128, 4, 128], F32R, 2)
            for t in range(ntiles):
                r0 = t * 128
                v, vk = vr.next(); u, uk = ur.next()
                self.ld(v[:], self.av[r0:r0 + 128, :], [], [vk], q='pool')
                self.ld(u[:], self.uT[:, :, r0:r0 + 128].rearrange("g p t -> p g t"), [], [uk], q='pool')
                pt, pk = self.ps()
                for g_ in range(4):
                    self.mm(pt[:, g_ * 128:(g_ + 1) * 128], v[:, g_ * 128:(g_ + 1) * 128], ws[:, g_, :], True, True, [vk, 'ws'], [pk])
                tm, tk = tr_.next()
                self.tt(tm[:], pt[:], bsb[:], ALU.add, [pk, 'bsb'], [tk])
                o, ok = orr.next()
                self.tt(o[:].rearrange("p g t -> p (g t)"), tm[:], u[:].rearrange("p g t -> p (g t)"), ALU.mult, [tk, uk], [ok])
                self.ld(self.mixT[0:4, :, r0:r0 + 128].rearrange("g p t -> p g t"), o[:], [ok], [('mixT', 'a', t)], force='pool')
            self.T.barrier()

    def gla(self, l, prescan):
        T = self.T
        with ExitStack() as st:
            self.eps_tile(st)
            mk = self.sb(st, 'mk', [128, 2, 512])
            self.ld(mk[:], self.masks[:, 0:2, :], [], ['mk'])
            S = [[self.sb(st, f'S{hp}{p}', [128, 128]) for p in range(2)] for hp in range(2)]
            par = [0, 0]
            oT = None if prescan else self.sb(st, 'oT', [128, 4, NTOK])
            kr = Ring(self, st, 'k', [128, 128], F32, 8); qr = Ring(self, st, 'q', [128, 128], F32, 8); gr = Ring(self, st, 'g', [128, 128], F32, 8)
            cr = Ring(self, st, 'cs', [128, 128], F32, 8); ar = Ring(self, st, 'A', [128, 128], F32, 8); br = Ring(self, st, 'B', [128, 128], F32, 8)
            e1r = Ring(self, st, 'e1', [128, 128], F32, 16); dr = Ring(self, st, 'dt', [128, 2], F32, 8)
            kir = Ring(self, st, 'ki', [128, 128], F32, 8); kor = Ring(self, st, 'ko', [128, 128], F32, 8)
            qzr = [Ring(self, st, f'qz{z}', [128, 128], F32, 8) for z in range(2)]
            kzr = [Ring(self, st, f'kz{z}', [128, 2, 128], F32, 4) for z in range(2)]
            for z in range(2):
                for (tz, kz_) in [(t_, (qzr[z].name, i_)) for i_, t_ in enumerate(qzr[z].t)] + [(t_, (kzr[z].name, i_)) for i_, t_ in enumerate(kzr[z].t)]:
                    T.op('dve', lambda e: e.memset(tz[:], 0.0), [], [kz_])
            vtr = Ring(self, st, 'vt', [128, 512], F32, 3)
            amr = Ring(self, st, 'am', [128, 512], F32, 3)
            osr = Ring(self, st, 'os', [128, 512], F32R, 2); sqr = Ring(self, st, 'sq', [128, 512], F32R, 2)
            rsr = Ring(self, st, 'rs', [128, 512], F32, 2); ogr = Ring(self, st, 'og', [128, 4, 128], F32, 2)
            bor = Ring(self, st, 'bo', [128, 4, 128], F32R, 2)

            def sk(hp):
                return ('S', hp, par[hp])

            def set_state(src_ap, rk):
                for hp in range(2):
                    par[hp] = 0
                    if src_ap is None:
                        T.op('dve', lambda e: e.memset(S[hp][0][:], 0.0), [], [sk(hp)])
                    else:
                        self.ld(S[hp][0][:], src_ap[:, hp * 128:(hp + 1) * 128], rk, [sk(hp)], q='pool')

            def run(tile_ids, d_, state_only, final):
                cur = prep(tile_ids[0], d_, state_only)
                for i_, t in enumerate(tile_ids):
                    nxt_ = prep(tile_ids[i_ + 1], d_, state_only) if i_ + 1 < len(tile_ids) else None
                    proc(t, d_, state_only, final, cur)
                    cur = nxt_

            def prep(t, d_, state_only):
                if True:
                    r0 = t * 128
                    qin = [None, None]; kin = [None, None]; dtot = [None, None]
                    kz = [kzr[z].next() for z in range(2)]
                    for hp in range(2):
                        k, kk = kr.next(); g_, gk = gr.next()
                        self.ld(k[:], self.kT[hp, :, r0:r0 + 128], [], [kk], q='pool')
                        self.ld(g_[:], self.gT[d_, hp, :, r0:r0 + 128], [], [gk], q='pool')
                        cs, ck_ = cr.next()
                        T.op('dve', lambda e: e.tensor_tensor_scan(out=cs[:], data0=self.rmask, data1=g_[:], initial=0.0, op0=ALU.mult, op1=ALU.add), [gk, 'cst'], [ck_])
                        A, Ak = ar.next(); B, Bk = br.next()
                        if d_ == 0:
                            for c in range(2):
                                self.ts(B[:, c * 64:(c + 1) * 64], cs[:, c * 64:(c + 1) * 64], -1.0, cs[:, c * 64 + 63:c * 64 + 64], ALU.mult, ALU.add, [ck_], [Bk])
                            Asrc, Akk = cs, ck_
                        else:
                            self.tt(B[:], cs[:], g_[:], ALU.subtract, [ck_, gk], [Bk])
                            for c in range(2):
                                self.ts(A[:, c * 64:(c + 1) * 64], B[:, c * 64:(c + 1) * 64], -1.0, cs[:, c * 64 + 63:c * 64 + 64], ALU.mult, ALU.add, [Bk, ck_], [Ak])
                            Asrc, Akk = A, Ak
                        dt_, dk = dr.next()
                        self.act(dt_[:], cs[:, 63:128:64], AF.Exp, [ck_], [dk])
                        dtot[hp] = (dt_, dk)
                        e3, e3k = e1r.next()
                        self.act(e3[:], B[:], AF.Exp, [Bk], [e3k])
                        ko, kok = kor.next()
                        self.tt(ko[:], k[:], e3[:], ALU.mult, [kk, e3k], [kok])
                        if not state_only:
                            q, qk = qr.next()
                            self.ld(q[:], self.qT[hp, :, r0:r0 + 128], [], [qk], q='pool')
                            e1, e1k = e1r.next(); e2, e2k = e1r.next()
                            self.act(e1[:], Asrc[:], AF.Exp, [Akk], [e1k])
                            self.act(e2[:], Asrc[:], AF.Exp, [Akk], [e2k], scale=-1.0)
                            ki, kik = kir.next()
                            qz = [qzr[z].next() for z in range(2)]
                            for z in range(2):
                                self.tt(qz[z][0][z * 64:(z + 1) * 64, :], q[z * 64:(z + 1) * 64, :], e1[z * 64:(z + 1) * 64, :], ALU.mult, [qk, e1k], [qz[z][1]])
                            self.tt(ki[:], k[:], e2[:], ALU.mult, [kk, e2k], [kik])
                            qin[hp] = qz; kin[hp] = (ki, kik)
                        ptt, ptk = self.ps()
                        T.op('pe', lambda e: e.transpose(out=ptt[:, 0:128], in_=ko[:], identity=self.ident), [kok, 'cst'], [ptk])
                        for z in range(2):
                            self.act(kz[z][0][z * 64:(z + 1) * 64, hp, :], ptt[z * 64:(z + 1) * 64, 0:128], AF.Copy, [ptk], [kz[z][1]])
                    return (qin, kin, dtot, kz)

            def proc(t, d_, state_only, final, P):
                chunk_order = (0, 1) if d_ == 0 else (1, 0)
                qin, kin, dtot, kz = P
                r0 = t * 128
                for _once in (0,):
                    v, vk = vtr.next()
                    self.ld(v[:], self.gv[r0:r0 + 128, :], [], [vk], q='pool')
                    pkvs = [self.ps() for hp in range(2)]
                    for hp in range(2):
                        for c in range(2):
                            self.mm(pkvs[hp][0][:, c * 256:(c + 1) * 256], kz[c][0][:, hp, :], v[:, hp * 256:(hp + 1) * 256], True, True, [kz[c][1], vk], [pkvs[hp][1]])
                    if not state_only:
                        pat, patk = self.ps()
                        for h in range(4):
                            hp, hh = h // 2, h % 2
                            self.mm(pat[:, h * 128:(h + 1) * 128], kin[hp][0][:, :], qin[hp][hh][0][:, :],
                                    True, True, [kin[hp][1], qin[hp][hh][1]], [patk])
                        am, amk = amr.next()
                        self.tt(am[:], pat[:], mk[:, d_, :], ALU.mult, [patk, 'mk'], [amk])
                        po, pok = self.ps()
                    if not state_only:
                        s_first = [(S[hp][par[hp]], sk(hp)) for hp in range(2)]
                    c0 = chunk_order[0]
                    for hp in range(2):
                        nxt = 1 - par[hp]
                        for z in range(2):
                            col = c0 * 256 + z * 128
                            self.stt(S[hp][nxt][z * 64:(z + 1) * 64, :], S[hp][par[hp]][z * 64:(z + 1) * 64, :], dtot[hp][0][z * 64:(z + 1) * 64, c0:c0 + 1],
                                     pkvs[hp][0][z * 64:(z + 1) * 64, col:col + 128], ALU.mult, ALU.add,
                                     [sk(hp), dtot[hp][1], pkvs[hp][1]], [('S', hp, nxt)])
                    if not state_only:
                        s_mid = [(S[hp][1 - par[hp]], ('S', hp, 1 - par[hp])) for hp in range(2)]
                        for h in range(4):
                            hp, hh = h // 2, h % 2
                            self.mm(po[:, h * 128:(h + 1) * 128], v[:, h * 128:(h + 1) * 128], am[:, h * 128:(h + 1) * 128], True, False, [vk, amk], [pok])
                            for ci, c in enumerate(chunk_order):
                                s_t, s_k = (s_first if ci == 0 else s_mid)[hp]
                                self.mm(po[:, h * 128 + c * 64:h * 128 + (c + 1) * 64], s_t[:, :],
                                        qin[hp][hh][0][:, c * 64:(c + 1) * 64], False, ci == 1, [s_k, qin[hp][hh][1]], [pok])
                    c1 = chunk_order[1]
                    for hp in range(2):
                        cur = 1 - par[hp]
                        for z in range(2):
                            col = c1 * 256 + z * 128
                            self.stt(S[hp][par[hp]][z * 64:(z + 1) * 64, :], S[hp][cur][z * 64:(z + 1) * 64, :], dtot[hp][0][z * 64:(z + 1) * 64, c1:c1 + 1],
                                     pkvs[hp][0][z * 64:(z + 1) * 64, col:col + 128], ALU.mult, ALU.add,
                                     [('S', hp, cur), dtot[hp][1], pkvs[hp][1]], [sk(hp)])
                    if state_only:
                        continue
                    if not final:
                        self.act(oT[:, :, r0:r0 + 128], po[:].rearrange("p (h t) -> p h t", h=4), AF.Copy, [pok], [('oT', t)])
                    else:
                        os_, osk = osr.next()
                        self.tt(os_[:].rearrange("p (h t) -> p h t", h=4), po[:].rearrange("p (h t) -> p h t", h=4), oT[:, :, r0:r0 + 128], ALU.add, [pok, ('oT', t)], [osk])
                        sq, sqk = sqr.next()
                        self.act(sq[:], os_[:], AF.Square, [osk], [sqk])
                        pn, pnk = self.ps()
                        self.mm(pn[:], self.ones, sq[:], True, True, [sqk, 'cstr'], [pnk])
                        rs, rsk = rsr.next()
                        self.act(rs[:], pn[:], AF.Sqrt, [pnk], [rsk], bias=self.epsD[:, 0:1], scale=1.0 / 128)
                        T.op('dve', lambda e: e.reciprocal(out=rs[:], in_=rs[:]), [rsk], [rsk])
                        self.tt(rs[:], rs[:], os_[:], ALU.mult, [rsk, osk], [rsk])
                        og, ogk = ogr.next()
                        self.ld(og[:], self.ogT[:, :, r0:r0 + 128].rearrange("h p t -> p h t"), [], [ogk], q='pool')
                        bo, bok = bor.next()
                        self.stt(bo[:].rearrange("p h t -> p (h t)"), rs[:], self.smg[:, l, 0:1], og[:].rearrange("p h t -> p (h t)"), ALU.mult, ALU.mult, [rsk, ogk, 'smg'], [bok])
                        self.ld(self.mixT[4:8, :, r0:r0 + 128].rearrange("h p t -> p h t"), bo[:], [bok], [('mixT', 'b', t)], force='pool')

            need_ctx = (l == 0)
            if prescan:
                set_state(None, [])
                run([NT, NT + 1], 0, True, False)
                run(list(range(NT)), 0, True, False)
                for hp in range(2 if GLA_DBG >= 4 else 0):
                    self.ld(self.export[:, hp * 128:(hp + 1) * 128], S[hp][par[hp]][:], [sk(hp)], [('exp', hp)], q='pool')
                for kv in range(2 if GLA_DBG >= 5 else 0):
                    self.ld(self.export[:, 256 + kv * 128:256 + (kv + 1) * 128], self.KTs.bitcast(F32)[kv, :, TL - 128:TL], [], [('exp', 2 + kv)], q='pool')
                if GLA_DBG >= 5:
                    self.ld(self.export[:, 512:768], self.Vs.bitcast(F32)[TL - 128:TL, :], [], [('exp', 4)], q='pool')
            else:
                set_state(None, [])
                run([NT, NT + 1], 0, not need_ctx, False)
                run(list(range(NT)), 0, False, False)
                if MODE == 'full':
                    set_state(None, [])
                    run([NT + 1, NT], 1, not need_ctx, True)
                else:
                    if need_ctx:
                        set_state(None, [])
                        run([NT + 1, NT], 1, False, True)
                    set_state(self.partner, [])
                run(list(range(NT - 1, -1, -1)), 1, False, True)
            T.barrier()

    def att(self, l):
        T = self.T
        need_ctx = (l == 0)
        with ExitStack() as st:
            mk = self.sb(st, 'mk', [128, 3, 512])
            self.ld(mk[:], self.masks[:, 2:5, :], [], ['mk'])
            Kb = self.sb(st, 'Kb', [128, 2, NT + 3, 128], F32R); Vb = self.sb(st, 'Vb', [128, NT + 3, 256], F32R)
            es_ = self.sb(st, 'es', [128, 8])
            self.ld(es_[:], self.sinkb[:, l, :], [], ['es'])
            self.act(es_[:], es_[:], AF.Exp, ['es'], ['es'])
            for kv in range(2):
                self.ld(Kb[:, kv, 0:NT, :], self.KTs[kv, :, 0:TL].rearrange("p (b t) -> p b t", t=128), [], [('Kb', kv)])
                self.ld(Kb[:, kv, NT + 1:NT + 3, :], self.KTs[kv, :, TL:NTOK].rearrange("p (b t) -> p b t", t=128), [], [('Kb', kv)])
                if MODE == 'pair':
                    self.ld(Kb[:, kv, NT, :], self.partner.bitcast(F32R)[:, 256 + kv * 128:256 + (kv + 1) * 128], [], [('Kb', kv)])
            self.ld(Vb[:, 0:NT, :], self.Vs[0:TL, :].rearrange("(b p) c -> p b c", p=128), [], ['Vb'])
            self.ld(Vb[:, NT + 1:NT + 3, :], self.Vs[TL:NTOK, :].rearrange("(b p) c -> p b c", p=128), [], ['Vb'])
            if MODE == 'pair':
                self.ld(Vb[:, NT, :], self.partner.bitcast(F32R)[:, 512:768], [], ['Vb'])
            qr = Ring(self, st, 'qt', [128, 8, 128], F32R, 2)
            pr = Ring(self, st, 'pT', [128, 512], F32R, 12)
            dr = Ring(self, st, 'den', [128, 512], F32, 3)
            cor = Ring(self, st, 'co', [128, 4, 128], F32R, 3)
            qtiles = list(range(NT)) + ([NT + 1, NT + 2] if need_ctx else [])
            for qi in qtiles:
                tok0 = qi * 128 if qi < NT else TL + (qi - NT - 1) * 128
                qt, qk = qr.next()
                self.ld(qt[:], self.QT[:, :, tok0:tok0 + 128].rearrange("h p t -> p h t"), [], [qk], q='pool')
                cblk = [(NT + 1, None), (NT + 2, None)]
                if qi < NT:
                    nxt = [(qi + 1, 1)] if qi < NT - 1 else ([(NT, 2)] if MODE == 'pair' else [])
                    blocks = ([(qi - 1, 0)] if qi > 0 else []) + [(qi, None)] + nxt + cblk
                else:
                    blocks = cblk
                for kv in range(2):
                    pTs = []
                    for bi, (kb, m) in enumerate(blocks):
                        ps_, psk = self.ps()
                        self.mm(ps_[:], Kb[:, kv, kb, :], qt[:, kv * 4:(kv + 1) * 4, :].rearrange("p h t -> p (h t)"), True, True, [('Kb', kv), qk], [psk])
                        pT, pTk = pr.next()
                        self.act(pT[:], ps_[:], AF.Exp, [psk], [pTk], scale=128 ** -0.5)
                        if m is not None:
                            self.tt(pT[:], pT[:], mk[:, m, :], ALU.mult, [pTk, 'mk'], [pTk])
                        pTs.append((pT, pTk, kb))
                    po, pok = self.ps(); pd, pdk = self.ps()
                    for bi, (pT, pTk, kb) in enumerate(pTs):
                        self.mm(po[:], Vb[:, kb, kv * 128:(kv + 1) * 128], pT[:], bi == 0, bi == len(blocks) - 1, ['Vb', pTk], [pok])
                        self.mm(pd[:], self.ones, pT[:], bi == 0, bi == len(blocks) - 1, ['cstr', pTk], [pdk])
                    den, dk = dr.next()
                    for h in range(4):
                        self.ts(den[:, h * 128:(h + 1) * 128], pd[:, h * 128:(h + 1) * 128], es_[:, kv * 4 + h:kv * 4 + h + 1], None, ALU.add, None, [pdk, 'es'], [dk])
                    T.op('dve', lambda e: e.reciprocal(out=den[:], in_=den[:]), [dk], [dk])
                    co, cok = cor.next()
                    self.tt(co[:].rearrange("p h t -> p (h t)"), po[:], den[:], ALU.mult, [pok, dk], [cok])
                    self.ld(self.mixT[8 + kv * 4:12 + kv * 4, :, tok0:tok0 + 128].rearrange("h p t -> p h t"), co[:], [cok], [('mixT', 'c', qi, kv)], q='pool')
            T.barrier()

    def merge(self, l, tiles):
        with ExitStack() as st:
            self.eps_tile(st)
            hTs = [self.sb(st, f'hT{i_}', [128, KT, 512], F32R) for i_ in range(2)]
            mt = self.sb(st, 'mt', [128, KT, 512], F32R)
            mg = self.sb(st, 'mg', [128, KT, 512], F32R)
            R = {'xs': Ring(self, st, 'xs', [128, 512], F32, 3), 'sq': Ring(self, st, 'sq', [128, 512], F32R, 2),
                 'tmp': Ring(self, st, 'tmp', [128, 512], F32, 2), 'rstd': self.sb(st, 'rstd', [128, 512])}
            wf = Ring(self, st, 'wf', [128, KT * 128], F32R, 4)
            sgr = Ring(self, st, 'sg', [128, 512], F32, 4)
            m1r = Ring(self, st, 'm1', [128, 512], F32, 3)
            xo_r = Ring(self, st, 'xo', [128, 512], F32, 2)
            def norm(ti):
                tok0, n = tiles[ti]
                self.norm_mod(R, self.xs, l, 1, 1 if tok0 >= TL else 0, tok0, n, hTs[ti % 2], hk=f'hT{ti % 2}')

            norm(0)
            for ti, (tok0, n) in enumerate(tiles):
                wch = 1 if tok0 >= TL else 0
                hT = hTs[ti % 2]; hk = f'hT{ti % 2}'
                self.ld(mt[:, :, :n], self.mixT[:, :, tok0:tok0 + n].rearrange("k p t -> p k t"), [], [('mt',)], q='pool')
                for nt in range(KT):
                    sgs = []
                    for b in range(3):
                        wt, wk = wf.next()
                        self.ld(wt[:], self.Wt('winF', l)[22 + b * 16 + nt, :, :], [], [wk])
                        pt, pk = self.ps()
                        for kt in range(KT):
                            self.mm(pt[:, :n], wt[:, kt * 128:(kt + 1) * 128], hT[:, kt, :n], kt == 0, kt == KT - 1, [wk, (hk, kt, 0)], [pk])
                        sg, sk = sgr.next()
                        self.act(sg[:, :n], pt[:, :n], AF.Sigmoid, [pk], [sk])
                        sgs.append((sg, sk))
                    wt, wk = wf.next()
                    self.ld(wt[:], self.Wt('wbr', l)[nt, :, :], [], [wk])
                    ms = []
                    for b, (k0, k1) in enumerate(((0, 4), (4, 8), (8, 16))):
                        pt, pk = self.ps()
                        for kt in range(k0, k1):
                            self.mm(pt[:, :n], wt[:, kt * 128:(kt + 1) * 128], mt[:, kt, :n], kt == k0, kt == k1 - 1, [wk, ('mt',)], [pk])
                        m1, m1k = m1r.next()
                        self.tt(m1[:, :n], sgs[b][0][:, :n], pt[:, :n], ALU.mult, [sgs[b][1], pk], [m1k])
                        ms.append((m1, m1k))
                    self.tt(ms[0][0][:, :n], ms[0][0][:, :n], ms[1][0][:, :n], ALU.add, [ms[0][1], ms[1][1]], [ms[0][1]])
                    self.tt(mg[:, nt, :n], ms[0][0][:, :n], ms[2][0][:, :n], ALU.add, [ms[0][1], ms[2][1]], [('mg', nt)])
                if ti + 1 < len(tiles):
                    norm(ti + 1)
                for nt in range(KT):
                    wt, wk = wf.next()
                    self.ld(wt[:], self.Wt('wout', l)[nt, :, :], [], [wk])
                    pt, pk = self.ps()
                    for kt in range(KT):
                        self.mm(pt[:, :n], wt[:, kt * 128:(kt + 1) * 128], mg[:, kt, :n], kt == 0, kt == KT - 1, [wk, ('mg', kt)], [pk])
                    xt, xk = R['xs'].next()
                    self.ld(xt[:, :n], self.xs[nt, :, tok0:tok0 + n], [('X', nt, tok0)], [xk], q='pool')
                    xo, ok = xo_r.next()
                    self.stt(xo[:, :n], pt[:, :n], self.nG[:, l, 1, wch, nt:nt + 1], xt[:, :n], ALU.mult, ALU.add, [pk, xk, 'nG'], [ok])
                    self.ld(self.xs[nt, :, tok0:tok0 + n], xo[:, :n], [ok], [('X', nt, tok0)], force='act')
            self.T.barrier()

    def part_a(self, l, src):
        self.ffn(l, 0, src, self.xs, TILES)
        if DEBUG_STOP == 'ffn1':
            return
        self.inproj(l, TILES, ('bk', 'bv', 'ck', 'cv', 'dec'))
        if DEBUG_STOP == 'inproj_a':
            return
        self.gla(l, True)

    def part_b(self, l, last):
        self.inproj(l, TILES, ('bk', 'bq', 'au', 'bg', 'ck', 'cq', 'dec', 'bv', 'cv', 'av'))
        if DEBUG_STOP == 'b_inproj':
            return
        self.mixa(l, NT + 2 if not last else NT)
        if DEBUG_STOP == 'b_mixa':
            return
        self.gla(l, False)
        if DEBUG_STOP == 'b_gla':
            return
        self.att(l)
        if DEBUG_STOP == 'b_att':
            return
        tiles = TILES[:-1] if last else TILES
        self.merge(l, tiles)
        if DEBUG_STOP == 'b_merge':
            return
        if last:
            self.ffn(l, 1, self.xs, self.y, tiles)
        else:
            self.ffn(l, 1, self.xs, self.xs, tiles)

    def build(self):
        self.setup()
        self.mods()
        L = self.launch
        if L == 'full':
            for l in range(NL):
                self.ffn(l, 0, self.x_in if l == 0 else self.xs, self.xs, TILES)
                self.part_b(l, l == NL - 1)
        elif L == 0:
            self.part_a(0, self.x_in)
        elif L == 1:
            self.copy_x()
            self.part_b(0, False)
            self.part_a(1, self.xs)
        else:
            self.copy_x()
            self.part_b(1, True)
        self.T.barrier()
        self.es.close()
        return self.nc

    def copy_x(self):
        for kt in range(KT):
            self.ld(self.xs[kt, :, :], self.x_in[kt, :, :], [], [('Xc', kt)], q='pool')
        self.T.barrier()


def _tile_fm(w, c0, ntile):
    K = w.shape[0]
    sub = w[:, c0:c0 + ntile * 128].reshape(K // 128, 128, ntile, 128)
    return np.ascontiguousarray(sub.transpose(2, 1, 0, 3)).reshape(ntile, 128, (K // 128) * 128)


def _prep_weights(inp):
    W = {}
    f32 = np.float32
    for l in range(NL):
        W[f'wada{l}'] = _tile_fm(inp['w_ada'][l], 0, 144)
        for f in range(2):
            w = inp['w_ffn_up'][l, f]
            wu = np.empty((FT, 128, 2, KT * 128), f32)
            wu[:, :, 0, :] = _tile_fm(w, 0, FT)
            wu[:, :, 1, :] = _tile_fm(w, FF, FT)
            W[f'wup{l}{f}'] = wu
            wd = inp['w_ffn_down'][l, f].reshape(4, 11, 128, KT, 128)
            W[f'wdn{l}{f}'] = np.ascontiguousarray(wd.transpose(3, 0, 2, 1, 4)).reshape(KT, 4, 128, 11 * 128)
        w = inp['w_in'][l]
        winF = np.empty((70, 128, KT * 128), f32)
        for nm, (t0, cnt) in FM.items():
            winF[t0:t0 + cnt] = _tile_fm(w, COLS[nm], cnt)
        W[f'winF{l}'] = winF
        tm = np.concatenate([w[:, 256:768], w[:, 1024:1280], w[:, 1792:2304]], axis=1)
        W[f'winT{l}'] = np.ascontiguousarray(tm.reshape(KT, 128, 1280).transpose(1, 0, 2))
        cat = np.concatenate([inp['w_br_a'][l], inp['w_br_b'][l], inp['w_br_c'][l]], axis=0)
        W[f'wbr{l}'] = _tile_fm(cat, 0, KT)
        W[f'wout{l}'] = _tile_fm(inp['w_out'][l], 0, KT)
    W['badaT'] = np.ascontiguousarray(inp['b_ada'].reshape(NL, 144, 128).transpose(2, 0, 1))
    W['normg'] = np.ascontiguousarray(inp['norm_g'].reshape(NL, 3, KT, 128).transpose(3, 0, 1, 2))
    W['avg'] = np.ascontiguousarray(np.broadcast_to(inp['a_v_gain'][None], (128, NL, 512)))
    sm = np.zeros((128, NL, 4), f32)
    sm[:, :, 0] = inp['b_norm_g'].T; sm[:, :, 1] = inp['c_q_gain'].T; sm[:, :, 2] = inp['c_k_gain'].T
    W['smallg'] = sm
    W['sinkb'] = np.ascontiguousarray(np.broadcast_to(inp['c_sink'][None], (128, NL, 8)))
    return W


def _prep_core_weights(inp, half):
    f32 = np.float32
    W = {}
    dirs = (0, 1) if half == 0 else (1, 0)
    bdec = np.zeros((128, NL, 2, 2), f32)
    ws = inp['a_ws']; bs = inp['a_bs']
    if half == 1:
        ws = ws[:, :, ::-1, ::-1]; bs = bs[:, :, ::-1]
    for l in range(NL):
        pad = np.zeros((D, 128), f32)
        wdec2 = np.zeros((128, 2, 256), f32)
        for di, dg in enumerate(dirs):
            pad[:, di * 16:(di + 1) * 16] = inp['b_decay_w1'][l, dg]
            wdec2[di * 16:(di + 1) * 16, di, :] = inp['b_decay_w2'][l, dg]
            bdec[:, l, di, :] = inp['b_decay_b'][l, dg].reshape(2, 128).T
        W[f'wdec1{l}'] = _tile_fm(pad, 0, 1)[0]
        W[f'wdec2{l}'] = wdec2
        W[f'awsT{l}'] = np.ascontiguousarray(ws[l].transpose(2, 0, 1))
    W['bdec'] = bdec
    W['absb'] = np.ascontiguousarray(np.broadcast_to(bs.reshape(NL, 512)[None], (128, NL, 512)))
    return W


def _consts():
    c = np.zeros((128, 6, 128), np.float32)
    c[:, 0, :] = 1.0
    c[:, 1, :] = np.eye(128, dtype=np.float32)
    Pm = np.zeros((128, 128), np.float32)
    for base in (0, 64):
        for d in range(32):
            Pm[base + d, base + d + 32] = -1.0
            Pm[base + 32 + d, base + d] = 1.0
    c[:, 2, :] = Pm.T
    rm = np.ones((128, 128), np.float32); rm[:, 0] = 0.0; rm[:, 64] = 0.0
    c[:, 3, :] = rm
    m = np.zeros((128, 5, 512), np.float32)
    s = np.arange(128)[:, None]; t = np.arange(128)[None, :]
    same = (s // 64) == (t // 64)
    tile4 = lambda a: np.tile(a.astype(np.float32), (1, 4))
    m[:, 0, :] = tile4(same & (s <= t)); m[:, 1, :] = tile4(same & (s >= t))
    m[:, 2, :] = tile4(s >= t); m[:, 3, :] = tile4(s <= t); m[:, 4, :] = tile4(s >= 127 - t)
    return c, m


def _rope_tables(half):
    pos = np.arange(TL) + (0 if half == 0 else TL)
    if half == 1:
        pos = pos[::-1]
    row = (pos // 64).astype(np.float32); col = (pos % 64).astype(np.float32)
    inv = (10000.0 ** (-np.arange(32, dtype=np.float32) / 32)).astype(np.float32)
    tab = np.zeros((128, 2, NTOK), np.float32)
    tab[:, 0, TL:] = 1.0
    for d in range(128):
        p = row if d < 64 else col
        ang = p * inv[d % 32]
        tab[d, 0, :TL] = np.cos(ang); tab[d, 1, :TL] = np.sin(ang)
    return tab


def _core_maps(inp, Wc, cst, msk, cores):
    per_half = {}
    ropes = {}
    maps = []
    for (b, h) in cores:
        hh = 0 if h is None else h
        if hh not in per_half:
            per_half[hh] = _prep_core_weights(inp, hh); ropes[hh] = _rope_tables(hh)
        if h is None:
            lat = inp['x'][b]; cx = inp['ctx'][b]
        else:
            lat = inp['x'][b, h * TL:(h + 1) * TL]; cx = inp['ctx'][b]
            if h == 1:
                lat = lat[::-1]; cx = cx[::-1]
        xl = np.concatenate([lat, cx], axis=0)
        m = dict(Wc); m.update(per_half[hh])
        m['consts'] = cst; m['masks'] = msk; m['rope'] = ropes[hh]
        cT = np.stack([inp['c'][b], inp['c_ctx']], axis=1).reshape(KT, 128, 2).transpose(1, 0, 2)
        m['cT'] = np.ascontiguousarray(cT)
        m['xT'] = np.ascontiguousarray(xl.T).reshape(KT, 128, NTOK)
        maps.append(m)
    return maps


def kernel(**inp):
    inp = {k: np.asarray(v, dtype=np.float32) for k, v in inp.items()}
    Wc = _prep_weights(inp)
    cst, msk = _consts()
    out = np.empty((4, 4096, D), np.float32)
    if MODE == 'full':
        base_maps = _core_maps(inp, Wc, cst, msk, [(b, None) for b in range(4)])
        kb = KB('full')
        nc = kb.build()
        maps = [{k: m[k] for k in kb.din} for m in base_maps]
        res = run_bass_kernel_spmd(nc, maps, core_ids=list(range(4)))
        for b in range(4):
            out[b] = np.asarray(res.results[b]['yT']).reshape(D, TL).T
        return out
    base_maps = _core_maps(inp, Wc, cst, msk, [(r // 2, r % 2) for r in range(NCORE)])
    state = None
    for launch in range(3):
        kb = KB(launch)
        nc = kb.build()
        maps = []
        for r in range(NCORE):
            src = dict(base_maps[r])
            if launch > 0:
                src['xs_in'] = state[r]['xs']; src['mod_in'] = state[r]['mod']; src['partner'] = state[r ^ 1]['export']
            maps.append({k: src[k] for k in kb.din})
        res = run_bass_kernel_spmd(nc, maps, core_ids=list(range(NCORE)))
        if launch < 2:
            new = []
            for r in range(NCORE):
                o = res.results[r]
                new.append({'xs': np.asarray(o['xs']), 'export': np.asarray(o['export']),
                            'mod': np.asarray(o['mod_out']) if launch == 0 else state[r]['mod']})
            state = new
    for r in range(NCORE):
        b, h = r // 2, r % 2
        y = np.asarray(res.results[r]['yT']).reshape(D, TL).T
        if h == 1:
            y = y[::-1]
        out[b, h * TL:(h + 1) * TL] = y
    return out
```

```python
import numpy as np
from contextlib import ExitStack
import concourse.bass as bass
import concourse.mybir as mybir
from concourse.bass_utils import run_bass_kernel_spmd

F32 = mybir.dt.float32
F32R = mybir.dt.float32r
AF = mybir.ActivationFunctionType
ALU = mybir.AluOpType

D = 2048; KT = 16; TC = 256; FF = 5632; FT = 44; NL = 2
MODE = 'pair'


def set_mode(mode):
    global MODE, TL, NTOK, NT, TILES
    MODE = mode
    TL = 4096 if mode == 'full' else 2048
    NTOK = TL + TC
    NT = TL // 128
    TILES = [(i * 512, 512) for i in range(TL // 512)] + [(TL, 256)]


NCORE = 8
EPS = 1e-6
set_mode(MODE)
FM = {'bk': (0, 2), 'bq': (2, 2), 'au': (4, 4), 'bg': (8, 4), 'ck': (12, 2), 'cq': (14, 8), 'gate': (22, 48)}
COLS = {'bk': 0, 'bv': 256, 'ck': 768, 'cv': 1024, 'au': 1280, 'av': 1792, 'bq': 2304, 'bg': 2560, 'cq': 3072, 'gate': 4096}
NQ = 12
DEBUG_STOP = None
GLA_DBG = 9


class Tr:
    def __init__(self, nc, es):
        self.nc = nc
        self.E = {'pe': nc.tensor, 'act': nc.scalar, 'dve': nc.vector, 'pool': nc.gpsimd, 'sp': nc.sync}
        self.sem = {}
        self.val = {}
        for e in self.E:
            self.sem[e] = es.enter_context(nc.semaphore('s_' + e)); self.val[e] = 0
        self.dq = {}
        for q in ('sp', 'pool', 'act'):
            ids = []
            for i in range(NQ):
                sid = f'd_{q}{i}'
                self.sem[sid] = es.enter_context(nc.semaphore(sid)); self.val[sid] = 0
                ids.append(sid)
            self.dq[q] = [ids, 0]
        self.seen = {e: {} for e in self.E}
        self.w = {}
        self.r = {}

    def _wait(self, e, sid, v):
        if v <= 0 or self.seen[e].get(sid, 0) >= v:
            return
        self.E[e].wait_ge(self.sem[sid], v)
        self.seen[e][sid] = v

    def _sync(self, e, r, w):
        for k in r:
            ev = self.w.get(k)
            if ev is not None and not (ev[0] == e == 'pe'):
                self._wait(e, ev[0], ev[1])
        for k in w:
            ev = self.w.get(k)
            if ev is not None and not (ev[0] == e == 'pe'):
                self._wait(e, ev[0], ev[1])
            for sid, v in self.r.get(k, {}).items():
                if not (sid == e == 'pe'):
                    self._wait(e, sid, v)

    def _record(self, ev, r, w):
        for k in r:
            d = self.r.setdefault(k, {})
            d[ev[0]] = max(d.get(ev[0], 0), ev[1])
        for k in w:
            self.w[k] = ev
            self.r[k] = {}

    def op(self, e, fn, r=(), w=()):
        self._sync(e, r, w)
        inst = fn(self.E[e])
        self.val[e] += 1
        inst.then_inc(self.sem[e], 1)
        self._record((e, self.val[e]), r, w)

    def dma(self, q, out, in_, r=(), w=()):
        ids, nxt = self.dq[q]
        sid = ids[nxt]; self.dq[q][1] = (nxt + 1) % NQ
        self._wait(q, sid, self.val[sid])
        self._sync(q, r, w)
        inst = self.E[q].dma_start(out=out, in_=in_)
        self.val[sid] += 16
        inst.then_inc(self.sem[sid], 16)
        self._record((sid, self.val[sid]), r, w)

    def barrier(self):
        for e in self.E:
            for sid, v in self.val.items():
                if sid == e == 'pe':
                    continue
                self._wait(e, sid, v)
        self.w.clear(); self.r.clear()


class Ring:
    def __init__(self, kb, st, name, shape, dt, n):
        kb.uid += 1
        self.t = [st.enter_context(kb.nc.sbuf_tensor(f'{name}{i}_{kb.uid}', shape, dt)) for i in range(n)]
        self.name = name; self.i = 0

    def next(self):
        i = self.i; self.i = (i + 1) % len(self.t)
        return self.t[i], (self.name, i)


class KB:
    def __init__(self, launch):
        self.launch = launch
        self.nc = nc = bass.Bass("TRN2", target_bir_lowering=False)
        self.es = ExitStack()
        self.T = Tr(nc, self.es)
        self.din = {}
        self.psb = [self.es.enter_context(nc.psum_tensor(f'ps{i}', [128, 512], F32)) for i in range(8)]
        self.psi = 0
        self.uid = 0

    def inp(self, name, shape, dt=F32):
        self.din[name] = self.nc.dram_tensor(name, list(shape), dt, kind="ExternalInput").ap()
        return self.din[name]

    def outp(self, name, shape, dt=F32):
        return self.nc.dram_tensor(name, list(shape), dt, kind="ExternalOutput").ap()

    def scr(self, name, shape, dt=F32):
        return self.nc.dram_tensor(name, list(shape), dt, kind="Internal").ap()

    def sb(self, st, name, shape, dt=F32):
        self.uid += 1
        return st.enter_context(self.nc.sbuf_tensor(f'{name}_{self.uid}', list(shape), dt))

    def ps(self):
        i = self.psi; self.psi = (i + 1) % 8
        return self.psb[i], ('ps', i)

    def mm(self, out, lhsT, rhs, start, stop, r, w):
        self.T.op('pe', lambda e: e.matmul(out, lhsT, rhs, start=start, stop=stop), r, w)

    def act(self, out, in_, func, r, w, bias=None, scale=None):
        kw = {}
        if bias is not None:
            kw['bias'] = bias
        if scale is not None:
            kw['scale'] = scale
        self.T.op('act', lambda e: e.activation(out=out, in_=in_, func=func, **kw), r, w)

    def tt(self, out, in0, in1, op, r, w, eng='dve'):
        self.T.op(eng, lambda e: e.tensor_tensor(out=out, in0=in0, in1=in1, op=op), r, w)

    def ts(self, out, in0, s1, s2, op0, op1, r, w, eng='dve'):
        if op1 is None:
            self.T.op(eng, lambda e: e.tensor_scalar(out=out, in0=in0, scalar1=s1, scalar2=None, op0=op0), r, w)
        else:
            self.T.op(eng, lambda e: e.tensor_scalar(out=out, in0=in0, scalar1=s1, scalar2=s2, op0=op0, op1=op1), r, w)

    def stt(self, out, in0, scalar, in1, op0, op1, r, w):
        self.T.op('dve', lambda e: e.scalar_tensor_tensor(out=out, in0=in0, scalar=scalar, in1=in1, op0=op0, op1=op1), r, w)

    def ld(self, out, in_, r, w, q=None, force=None):
        if out.dtype == F32R and in_.dtype == F32R and str(out.space).endswith('DRAM'):
            out = out.bitcast(F32); in_ = in_.bitcast(F32)
        q = 'pool' if (out.dtype == F32R or in_.dtype == F32R) else 'sp'
        if force is not None and q == 'sp':
            q = force
        shp = tuple(out.shape)
        if len(shp) == 3 and shp[1] > 4 and tuple(in_.shape) == shp:
            for a in range(0, shp[1], 4):
                b = min(shp[1], a + 4)
                self.T.dma(q, out[:, a:b, :], in_[:, a:b, :], r, w)
            return
        self.T.dma(q, out, in_, r, w)

    LAZY = {'wada': ([144, 128, KT * 128], 1), 'wup': ([FT, 128, 2, KT * 128], 2), 'wdn': ([KT, 4, 128, 11 * 128], 2),
            'winF': ([70, 128, KT * 128], 1), 'winT': ([128, KT, 1280], 1), 'wdec1': ([128, KT * 128], 1),
            'wdec2': ([128, 2, 256], 1), 'awsT': ([128, 4, 128], 1), 'wbr': ([KT, 128, KT * 128], 1), 'wout': ([KT, 128, KT * 128], 1)}

    def Wt(self, name, *idx):
        key = name + ''.join(str(i) for i in idx)
        if key not in self._lazy:
            self._lazy[key] = self.inp(key, self.LAZY[name][0], F32R)
        return self._lazy[key]

    def setup(self):
        nc = self.nc; es = self.es
        L = self.launch
        self.x_in = self.inp('xT', [KT, 128, NTOK]) if L in (0, 'full') else self.inp('xs_in', [KT, 128, NTOK])
        self.cT = self.inp('cT', [128, KT, 2])
        self.consts = self.inp('consts', [128, 6, 128])
        self.masks = self.inp('masks', [128, 5, 512])
        self.rope = self.inp('rope', [128, 2, NTOK])
        self._lazy = {}
        self.normg = self.inp('normg', [128, NL, 3, KT])
        self.badaT = self.inp('badaT', [128, NL, 144])
        self.bdec = self.inp('bdec', [128, NL, 2, 2])
        self.avg = self.inp('avg', [128, NL, 512])
        self.absb = self.inp('absb', [128, NL, 512])
        self.smallg = self.inp('smallg', [128, NL, 4])
        self.sinkb = self.inp('sinkb', [128, NL, 8])
        if L in (1, 2):
            self.mod_in = self.inp('mod_in', [128, NL * 144 * 2])
            self.partner = self.inp('partner', [128, 768])
        self.xs = self.outp('xs', [KT, 128, NTOK]) if L in (0, 1) else self.scr('xs', [KT, 128, NTOK])
        if L in (2, 'full'):
            self.y = self.outp('yT', [KT, 128, TL])
        if L in (0, 1):
            self.export = self.outp('export', [128, 768])
        if L == 0:
            self.mod_out = self.outp('mod_out', [128, NL * 144 * 2])
        S = self.scr
        self.kT = S('kT', [2, 128, NTOK]); self.qT = S('qT', [2, 128, NTOK]); self.gT = S('gT', [2, 2, 128, NTOK])
        self.gv = S('gv', [NTOK, 512]); self.ogT = S('ogT', [4, 128, NTOK])
        self.QT = S('QT', [8, 128, NTOK], F32R); self.KTs = S('KTs', [2, 128, NTOK], F32R); self.Vs = S('Vs', [NTOK, 256], F32R)
        self.uT = S('uT', [4, 128, NTOK]); self.av = S('av', [NTOK, 512], F32R)
        self.mixT = S('mixT', [KT, 128, NTOK], F32R)
        g = lambda n, s, d=F32: es.enter_context(nc.sbuf_tensor(n, s, d))
        self.cst = g('cst', [128, 6, 128])
        self.cstr = g('cstr', [128, 3, 128], F32R)
        self.modraw = g('modraw', [128, NL, 144, 2])
        self.nA = g('nA', [128, NL, 3, 2, KT]); self.nB = g('nB', [128, NL, 3, 2, KT]); self.nG = g('nG', [128, NL, 3, 2, KT])
        self.ngt = g('ngt', [128, NL, 3, KT])
        self.smg = g('smg', [128, NL, 4])
        self.ld(self.cst[:], self.consts[:, :, :], [], ['cst'])
        self.ld(self.ngt[:], self.normg[:, :, :, :], [], ['ngt'])
        self.ld(self.smg[:], self.smallg[:, :, :], [], ['smg'])
        self.T.op('dve', lambda e: e.tensor_copy(out=self.cstr[:], in_=self.cst[:, 0:3, :]), ['cst'], ['cstr'])
        self.ones = self.cstr[:, 0, :]; self.ident = self.cst[:, 1, :]; self.ropeP = self.cstr[:, 2, :]
        self.rmask = self.cst[:, 3, :]

    def mods(self):
        T = self.T
        with ExitStack() as st:
            if self.launch in (0, 'full'):
                craw = self.sb(st, 'craw', [128, KT, 2]); cact = self.sb(st, 'cact', [128, KT, 2], F32R)
                bad = self.sb(st, 'bad', [128, NL, 144])
                self.ld(craw[:], self.cT[:, :, :], [], ['craw'])
                self.ld(bad[:], self.badaT[:, :, :], [], ['bad'])
                self.act(cact[:], craw[:], AF.Silu, ['craw'], ['cact'])
                wr = Ring(self, st, 'wad', [128, KT * 128], F32R, 4)
                for l in range(NL):
                    pt, pk = self.ps()
                    for j in range(144):
                        wt, wk = wr.next()
                        self.ld(wt[:], self.Wt('wada', l)[j, :, :], [], [wk])
                        for kt in range(KT):
                            self.mm(pt[:, 2 * j:2 * j + 2], wt[:, kt * 128:(kt + 1) * 128], cact[:, kt, :],
                                    kt == 0, kt == KT - 1, [wk, 'cact'], [pk])
                    for wch in range(2):
                        self.tt(self.modraw[:, l, :, wch], pt[:, wch:288:2], bad[:, l, :], ALU.add, [pk, 'bad'], [('modraw', l, wch)])
                T.barrier()
                if self.launch == 0:
                    self.ld(self.mod_out[:, :], self.modraw[:].rearrange("p l j w -> p (l j w)"), [], ['mod_out'], q='pool')
            else:
                self.ld(self.modraw[:].rearrange("p l j w -> p (l j w)"), self.mod_in[:, :], [], [('modraw', l, w) for l in range(NL) for w in range(2)])
            for l in range(NL):
                for i in range(3):
                    for wch in range(2):
                        rk = [('modraw', l, wch), 'ngt']
                        sc = self.modraw[:, l, (3 * i + 1) * KT:(3 * i + 2) * KT, wch]
                        self.stt(self.nA[:, l, i, wch, :], sc, 1.0, self.ngt[:, l, i, :], ALU.add, ALU.mult, rk, ['nA'])
                        self.T.op('dve', lambda e: e.tensor_copy(out=self.nB[:, l, i, wch, :], in_=self.modraw[:, l, (3 * i) * KT:(3 * i + 1) * KT, wch]), rk, ['nB'])
                        self.ts(self.nG[:, l, i, wch, :], self.modraw[:, l, (3 * i + 2) * KT:(3 * i + 3) * KT, wch],
                                (1.0 if i == 1 else 0.5), None, ALU.mult, None, rk, ['nG'])
            T.barrier()

    def norm_mod(self, R, src, l, i, wch, tok0, n, hT, hoff=0):
        xs_r, sq_r, tmp_r = R['xs'], R['sq'], R['tmp']
        rstd = R['rstd']
        pt, pk = self.ps()
        for kt in range(KT):
            xt, xk = xs_r.next()
            self.ld(xt[:, :n], src[kt, :, tok0:tok0 + n], [('X', kt, tok0)], [xk], q='pool')
            sq, sk = sq_r.next()
            self.act(sq[:, :n], xt[:, :n], AF.Square, [xk], [sk])
            self.mm(pt[:, :n], self.ones, sq[:, :n], kt == 0, kt == KT - 1, [sk, 'cstr'], [pk])
        self.act(rstd[:, :n], pt[:, :n], AF.Sqrt, [pk], ['rstd'], bias=self.epsD[:, 0:1], scale=1.0 / D)
        self.T.op('dve', lambda e: e.reciprocal(out=rstd[:, :n], in_=rstd[:, :n]), ['rstd'], ['rstd'])
        for kt in range(KT):
            xt, xk = xs_r.next()
            self.ld(xt[:, :n], src[kt, :, tok0:tok0 + n], [('X', kt, tok0)], [xk], q='pool')
            tm, tk = tmp_r.next()
            self.tt(tm[:, :n], xt[:, :n], rstd[:, :n], ALU.mult, [xk, 'rstd'], [tk])
            self.act(hT[:, kt, hoff:hoff + n], tm[:, :n], AF.Identity, [tk, 'nA', 'nB'], [('hT', kt, hoff)],
                     bias=self.nB[:, l, i, wch, kt:kt + 1], scale=self.nA[:, l, i, wch, kt:kt + 1])

    def eps_tile(self, st):
        self.epsD = self.sb(st, 'epsD', [128, 2])
        self.T.op('dve', lambda e: e.memset(self.epsD[:], EPS), [], ['epsD'])

    def ffn(self, l, f, src, dst, tiles, dst_tok_off=0):
        i = 0 if f == 0 else 2
        QH = FT // 4
        ftiles = []
        for (t0, n) in tiles:
            if n == 512 and ftiles and ftiles[-1][1] == 512 and ftiles[-1][0] + 512 == t0:
                ftiles[-1] = (ftiles[-1][0], 1024)
            else:
                ftiles.append((t0, n))
        with ExitStack() as st:
            self.eps_tile(st)
            aT = self.sb(st, 'aT', [128, QH, 1024], F32R)
            hT = self.sb(st, 'hT', [128, KT, 1024], F32R)
            R = {'xs': Ring(self, st, 'xs', [128, 512], F32, 3), 'sq': Ring(self, st, 'sq', [128, 512], F32R, 2),
                 'tmp': Ring(self, st, 'tmp', [128, 512], F32, 2), 'rstd': self.sb(st, 'rstd', [128, 512])}
            wu = Ring(self, st, 'wu', [128, 2, KT * 128], F32R, 2)
            wd = Ring(self, st, 'wd', [128, QH * 128], F32R, 3)
            sg_r = Ring(self, st, 'sg', [128, 512], F32, 2)
            xr_r = Ring(self, st, 'xr', [128, 512], F32, 3)
            xo_r = Ring(self, st, 'xo', [128, 512], F32, 3)

            def halves(n):
                return [(0, min(n, 512))] + ([(512, n - 512)] if n > 512 else [])

            def norm(ti):
                tok0, n = ftiles[ti]
                for (h0, hn) in halves(n):
                    self.norm_mod(R, src, l, i, 1 if tok0 >= TL else 0, tok0 + h0, hn, hT, hoff=h0)

            def up(ti, q):
                tok0, n = ftiles[ti]
                for jj in range(QH):
                    j = q * QH + jj
                    wt, wk = wu.next()
                    self.ld(wt[:], self.Wt('wup', l, f)[j, :, :, :], [], [wk])
                    for (h0, hn) in halves(n):
                        pg, pgk = self.ps(); pv, pvk = self.ps()
                        for kt in range(KT):
                            self.mm(pg[:, :hn], wt[:, 0, kt * 128:(kt + 1) * 128], hT[:, kt, h0:h0 + hn], kt == 0, kt == KT - 1, [wk, ('hT', kt, h0)], [pgk])
                        for kt in range(KT):
                            self.mm(pv[:, :hn], wt[:, 1, kt * 128:(kt + 1) * 128], hT[:, kt, h0:h0 + hn], kt == 0, kt == KT - 1, [wk, ('hT', kt, h0)], [pvk])
                        sg, sk = sg_r.next()
                        self.act(sg[:, :hn], pg[:, :hn], AF.Silu, [pgk], [sk])
                        self.tt(aT[:, jj, h0:h0 + hn], sg[:, :hn], pv[:, :hn], ALU.mult, [sk, pvk], [('aT', jj, h0)])

            def down(ti, q, nts):
                tok0, n = ftiles[ti]; wch = 1 if tok0 >= TL else 0
                for nt in nts:
                    wt, wk = wd.next()
                    self.ld(wt[:], self.Wt('wdn', l, f)[nt, q, :, :], [], [wk])
                    for (h0, hn) in halves(n):
                        pt, pk = self.ps()
                        for kk in range(QH):
                            self.mm(pt[:, :hn], wt[:, kk * 128:(kk + 1) * 128], aT[:, kk, h0:h0 + hn], kk == 0, kk == QH - 1, [wk, ('aT', kk, h0)], [pk])
                        xt, xk = xr_r.next()
                        rsrc = src if q == 0 else dst
                        roff = 0 if q == 0 else dst_tok_off
                        t_in = tok0 + h0 - roff; t_out = tok0 + h0 - dst_tok_off
                        self.ld(xt[:, :hn], rsrc[nt, :, t_in:t_in + hn], [('X', q == 0 and src is not dst, nt, tok0 + h0)], [xk])
                        xo, ok = xo_r.next()
                        self.stt(xo[:, :hn], pt[:, :hn], self.nG[:, l, i, wch, nt:nt + 1], xt[:, :hn], ALU.mult, ALU.add, [pk, xk, 'nG'], [ok])
                        self.ld(dst[nt, :, t_out:t_out + hn], xo[:, :hn], [ok], [('X', False, nt, tok0 + h0)], force='act')

            norm(0)
            for ti in range(len(ftiles)):
                for q in range(4):
                    up(ti, q)
                    if q == 3 and ti + 1 < len(ftiles):
                        down(ti, q, range(0, 8))
                        norm(ti + 1)
                        down(ti, q, range(8, 16))
                    else:
                        down(ti, q, range(KT))
            self.T.barrier()

    def inproj(self, l, tiles, names):
        with ExitStack() as st:
            self.eps_tile(st)
            hT = self.sb(st, 'hT', [128, KT, 512], F32R)
            R = {'xs': Ring(self, st, 'xs', [128, 512], F32, 3), 'sq': Ring(self, st, 'sq', [128, 512], F32R, 2),
                 'tmp': Ring(self, st, 'tmp', [128, 512], F32, 2), 'rstd': self.sb(st, 'rstd', [128, 512])}
            wf = Ring(self, st, 'wf', [128, KT * 128], F32R, 4)
            wtm = Ring(self, st, 'wtm', [128, KT, 512], F32R, 2)
            so = Ring(self, st, 'so', [128, 512], F32, 12)
            sr = Ring(self, st, 'sr', [128, 512], F32R, 9)
            w2 = self.sb(st, 'w2', [128, 2, 256], F32R); bd = self.sb(st, 'bd', [128, 2, 2])
            avg = self.sb(st, 'avg', [128, 512]); rp = self.sb(st, 'rp', [128, 2, 512])
            bst = self.sb(st, 'bst', [128, 8]); mv = self.sb(st, 'mv', [128, 4])
            self.ld(w2[:], self.Wt('wdec2', l)[:, :, :], [], ['w2']); self.ld(bd[:], self.bdec[:, l, :, :], [], ['bd'])
            self.ld(avg[:], self.avg[:, l, :], [], ['avg'])

            def fm(ti, widx):
                tok0, n = tiles[ti]
                wt, wk = wf.next()
                self.ld(wt[:], widx, [], [wk])
                pt, pk = self.ps()
                for kt in range(KT):
                    self.mm(pt[:, :n], wt[:, kt * 128:(kt + 1) * 128], hT[:, kt, :n], kt == 0, kt == KT - 1, [wk, ('hT', kt, 0)], [pk])
                return pt, pk

            for ti in range(len(tiles)):
                tok0, n = tiles[ti]; wch = 1 if tok0 >= TL else 0
                self.norm_mod(R, self.xs, l, 1, wch, tok0, n, hT)
                if 'ck' in names:
                    self.ld(rp[:, :, :n], self.rope[:, :, tok0:tok0 + n], [('rp',)], ['rp'], q='pool')
                for nm in ('bk', 'bq', 'au', 'bg'):
                    if nm not in names:
                        continue
                    t0, cnt = FM[nm]
                    for j in range(cnt):
                        pt, pk = fm(ti, self.Wt('winF', l)[t0 + j, :, :])
                        o, ok = so.next()
                        if nm == 'bk':
                            self.act(o[:, :n], pt[:, :n], AF.Copy, [pk], [ok])
                            dstap = self.kT[j, :, tok0:tok0 + n]
                        elif nm == 'bq':
                            self.act(o[:, :n], pt[:, :n], AF.Copy, [pk], [ok], scale=0.125)
                            dstap = self.qT[j, :, tok0:tok0 + n]
                        elif nm == 'au':
                            self.act(o[:, :n], pt[:, :n], AF.Gelu_apprx_tanh, [pk], [ok])
                            dstap = self.uT[j, :, tok0:tok0 + n]
                        else:
                            self.act(o[:, :n], pt[:, :n], AF.Silu, [pk], [ok])
                            dstap = self.ogT[j, :, tok0:tok0 + n]
                        self.ld(dstap, o[:, :n], [ok], [(nm, j, tok0)], q='pool')
                for nm in ('ck', 'cq'):
                    if nm not in names:
                        continue
                    t0, cnt = FM[nm]
                    gcol = self.smg[:, l, 2:3] if nm == 'ck' else self.smg[:, l, 1:2]
                    for j in range(cnt):
                        pt, pk = fm(ti, self.Wt('winF', l)[t0 + j, :, :])
                        qg, qk = sr.next(); sq, sk = sr.next()
                        self.act(qg[:, :n], pt[:, :n], AF.Identity, [pk, 'smg'], [qk], scale=gcol)
                        self.act(sq[:, :n], pt[:, :n], AF.Square, [pk], [sk])
                        p2, p2k = self.ps(); p3, p3k = self.ps()
                        self.mm(p2[:, :n], self.ones, sq[:, :n], True, True, [sk, 'cstr'], [p2k])
                        self.mm(p3[:, :n], self.ropeP, qg[:, :n], True, True, [qk, 'cstr'], [p3k])
                        rs, rk = so.next()
                        self.act(rs[:, :n], p2[:, :n], AF.Sqrt, [p2k], [rk], bias=self.epsD[:, 0:1], scale=1.0 / 128)
                        self.T.op('dve', lambda e: e.reciprocal(out=rs[:, :n], in_=rs[:, :n]), [rk], [rk])
                        t1, t1k = so.next(); t2, t2k = so.next()
                        self.tt(t1[:, :n], qg[:, :n], rp[:, 0, :n], ALU.mult, [qk, 'rp'], [t1k])
                        self.tt(t2[:, :n], p3[:, :n], rp[:, 1, :n], ALU.mult, [p3k, 'rp'], [t2k])
                        self.tt(t1[:, :n], t1[:, :n], t2[:, :n], ALU.add, [t1k, t2k], [t1k])
                        o, ok = sr.next()
                        self.tt(o[:, :n], t1[:, :n], rs[:, :n], ALU.mult, [t1k, rk], [ok])
                        dstap = (self.KTs if nm == 'ck' else self.QT)[j, :, tok0:tok0 + n]
                        self.ld(dstap, o[:, :n], [ok], [(nm, j, tok0)], q='pool')
                if 'dec' in names:
                    pt, pk = fm(ti, self.Wt('wdec1', l)[:, :])
                    rT, rTk = sr.next()
                    self.act(rT[:, :n], pt[:, :n], AF.Copy, [pk], [rTk])
                    for d_ in range(2):
                        for hp in range(2):
                            p2, p2k = self.ps()
                            self.mm(p2[:, :n], w2[:, d_, hp * 128:(hp + 1) * 128], rT[:, :n], True, True, ['w2', rTk], [p2k])
                            o, ok = so.next()
                            self.act(o[:, :n], p2[:, :n], AF.Sigmoid, [p2k, 'bd'], [ok], bias=bd[:, d_, hp:hp + 1])
                            self.act(o[:, :n], o[:, :n], AF.Ln, [ok], [ok])
                            o2, o2k = so.next()
                            self.ts(o2[:, :n], o[:, :n], 1.0 / 16.0, None, ALU.mult, None, [ok], [o2k])
                            self.ld(self.gT[d_, hp, :, tok0:tok0 + n], o2[:, :n], [o2k], [('gT', d_, hp, tok0)], q='pool')
                for nm, c0, nc_, dst in (('bv', 0, 512, self.gv), ('cv', 512, 256, self.Vs), ('av', 768, 512, self.av)):
                    if nm not in names:
                        continue
                    wt, wk = wtm.next()
                    self.ld(wt[:, :, :nc_], self.Wt('winT', l)[:, :, c0:c0 + nc_], [], [wk])
                    for sub in range(n // 128):
                        pt, pk = self.ps()
                        for kt in range(KT):
                            self.mm(pt[:, :nc_], hT[:, kt, sub * 128:(sub + 1) * 128], wt[:, kt, :nc_], kt == 0, kt == KT - 1, [wk, ('hT', kt, 0)], [pk])
                        r0 = tok0 + sub * 128
                        if nm == 'bv':
                            o, ok = so.next()
                            self.act(o[:, :nc_], pt[:, :nc_], AF.Copy, [pk], [ok])
                        elif nm == 'cv':
                            o, ok = sr.next()
                            self.act(o[:, :nc_], pt[:, :nc_], AF.Copy, [pk], [ok])
                        else:
                            g_, gk = so.next()
                            self.act(g_[:, :], pt[:, :], AF.Gelu_apprx_tanh, [pk], [gk])
                            self.T.op('dve', lambda e: e.bn_stats(out=bst[:, 0:6], in_=g_[:, :]), [gk], ['bst'])
                            self.T.op('dve', lambda e: e.bn_aggr(out=mv[:, 0:2], in_=bst[:, 0:6]), ['bst'], ['mv'])
                            self.act(mv[:, 2:3], mv[:, 1:2], AF.Sqrt, ['mv'], ['mv'], bias=self.epsD[:, 0:1], scale=1.0)
                            self.T.op('dve', lambda e: e.reciprocal(out=mv[:, 3:4], in_=mv[:, 2:3]), ['mv'], ['mv'])
                            self.ts(g_[:, :], g_[:, :], mv[:, 0:1], mv[:, 3:4], ALU.subtract, ALU.mult, [gk, 'mv'], [gk])
                            o, ok = sr.next()
                            self.tt(o[:, :], g_[:, :], avg[:, :], ALU.mult, [gk, 'avg'], [ok])
                        self.ld(dst[r0:r0 + 128, :], o[:, :nc_], [ok], [(nm, r0)], q='pool')
            self.T.barrier()

    def mixa(self, l, ntiles):
        with ExitStack() as st:
            ws = self.sb(st, 'ws', [128, 4, 128], F32R); bsb = self.sb(st, 'bsb', [128, 512])
            self.ld(ws[:], self.Wt('awsT', l)[:, :, :], [], ['ws']); self.ld(bsb[:], self.absb[:, l, :], [], ['bsb'])
            vr = Ring(self, st, 'vr', [128, 512], F32R, 2); ur = Ring(self, st, 'ur', [128, 4, 128], F32, 2)
            tr_ = Ring(self, st, 'tr', [128, 512], F32, 2); orr = Ring(self, st, 'or', [128, 4, 128], F32R, 2)
            for t in range(ntiles):
                r0 = t * 128
                v, vk = vr.next(); u, uk = ur.next()
                self.ld(v[:], self.av[r0:r0 + 128, :], [], [vk], q='pool')
                self.ld(u[:], self.uT[:, :, r0:r0 + 128].rearrange("g p t -> p g t"), [], [uk], q='pool')
                pt, pk = self.ps()
                for g_ in range(4):
                    self.mm(pt[:, g_ * 128:(g_ + 1) * 128], v[:, g_ * 128:(g_ + 1) * 128], ws[:, g_, :], True, True, [vk, 'ws'], [pk])
                tm, tk = tr_.next()
                self.tt(tm[:], pt[:], bsb[:], ALU.add, [pk, 'bsb'], [tk])
                o, ok = orr.next()
                self.tt(o[:].rearrange("p g t -> p (g t)"), tm[:], u[:].rearrange("p g t -> p (g t)"), ALU.mult, [tk, uk], [ok])
                self.ld(self.mixT[0:4, :, r0:r0 + 128].rearrange("g p t -> p g t"), o[:], [ok], [('mixT', 'a', t)], force='pool')
            self.T.barrier()

    def gla(self, l, prescan):
        T = self.T
        with ExitStack() as st:
            self.eps_tile(st)
            mk = self.sb(st, 'mk', [128, 2, 512])
            self.ld(mk[:], self.masks[:, 0:2, :], [], ['mk'])
            S = [[self.sb(st, f'S{hp}{p}', [128, 128]) for p in range(2)] for hp in range(2)]
            par = [0, 0]
            oT = None if prescan else self.sb(st, 'oT', [128, 4, NTOK])
            kr = Ring(self, st, 'k', [128, 128], F32, 8); qr = Ring(self, st, 'q', [128, 128], F32, 8); gr = Ring(self, st, 'g', [128, 128], F32, 8)
            cr = Ring(self, st, 'cs', [128, 128], F32, 8); ar = Ring(self, st, 'A', [128, 128], F32, 8); br = Ring(self, st, 'B', [128, 128], F32, 8)
            e1r = Ring(self, st, 'e1', [128, 128], F32, 16); dr = Ring(self, st, 'dt', [128, 2], F32, 8)
            kir = Ring(self, st, 'ki', [128, 128], F32, 8); kor = Ring(self, st, 'ko', [128, 128], F32, 8)
            qzr = [Ring(self, st, f'qz{z}', [128, 128], F32, 8) for z in range(2)]
            kzr = [Ring(self, st, f'kz{z}', [128, 2, 128], F32, 4) for z in range(2)]
            for z in range(2):
                for (tz, kz_) in [(t_, (qzr[z].name, i_)) for i_, t_ in enumerate(qzr[z].t)] + [(t_, (kzr[z].name, i_)) for i_, t_ in enumerate(kzr[z].t)]:
                    T.op('dve', lambda e: e.memset(tz[:], 0.0), [], [kz_])
            vtr = Ring(self, st, 'vt', [128, 512], F32, 3)
            amr = Ring(self, st, 'am', [128, 512], F32, 3)
            osr = Ring(self, st, 'os', [128, 512], F32R, 2); sqr = Ring(self, st, 'sq', [128, 512], F32R, 2)
            rsr = Ring(self, st, 'rs', [128, 512], F32, 2); ogr = Ring(self, st, 'og', [128, 4, 128], F32, 2)
            bor = Ring(self, st, 'bo', [128, 4, 128], F32R, 2)

            def sk(hp):
                return ('S', hp, par[hp])

            def set_state(src_ap, rk):
                for hp in range(2):
                    par[hp] = 0
                    if src_ap is None:
                        T.op('dve', lambda e: e.memset(S[hp][0][:], 0.0), [], [sk(hp)])
                    else:
                        self.ld(S[hp][0][:], src_ap[:, hp * 128:(hp + 1) * 128], rk, [sk(hp)], q='pool')

            def run(tile_ids, d_, state_only, final):
                cur = prep(tile_ids[0], d_, state_only)
                for i_, t in enumerate(tile_ids):
                    nxt_ = prep(tile_ids[i_ + 1], d_, state_only) if i_ + 1 < len(tile_ids) else None
                    proc(t, d_, state_only, final, cur)
                    cur = nxt_

            def prep(t, d_, state_only):
                if True:
                    r0 = t * 128
                    qin = [None, None]; kin = [None, None]; dtot = [None, None]
                    kz = [kzr[z].next() for z in range(2)]
                    for hp in range(2):
                        k, kk = kr.next(); g_, gk = gr.next()
                        self.ld(k[:], self.kT[hp, :, r0:r0 + 128], [], [kk], q='pool')
                        self.ld(g_[:], self.gT[d_, hp, :, r0:r0 + 128], [], [gk], q='pool')
                        cs, ck_ = cr.next()
                        T.op('dve', lambda e: e.tensor_tensor_scan(out=cs[:], data0=self.rmask, data1=g_[:], initial=0.0, op0=ALU.mult, op1=ALU.add), [gk, 'cst'], [ck_])
                        A, Ak = ar.next(); B, Bk = br.next()
                        if d_ == 0:
                            for c in range(2):
                                self.ts(B[:, c * 64:(c + 1) * 64], cs[:, c * 64:(c + 1) * 64], -1.0, cs[:, c * 64 + 63:c * 64 + 64], ALU.mult, ALU.add, [ck_], [Bk])
                            Asrc, Akk = cs, ck_
                        else:
                            self.tt(B[:], cs[:], g_[:], ALU.subtract, [ck_, gk], [Bk])
                            for c in range(2):
                                self.ts(A[:, c * 64:(c + 1) * 64], B[:, c * 64:(c + 1) * 64], -1.0, cs[:, c * 64 + 63:c * 64 + 64], ALU.mult, ALU.add, [Bk, ck_], [Ak])
                            Asrc, Akk = A, Ak
                        dt_, dk = dr.next()
                        self.act(dt_[:], cs[:, 63:128:64], AF.Exp, [ck_], [dk])
                        dtot[hp] = (dt_, dk)
                        e3, e3k = e1r.next()
                        self.act(e3[:], B[:], AF.Exp, [Bk], [e3k])
                        ko, kok = kor.next()
                        self.tt(ko[:], k[:], e3[:], ALU.mult, [kk, e3k], [kok])
                        if not state_only:
                            q, qk = qr.next()
                            self.ld(q[:], self.qT[hp, :, r0:r0 + 128], [], [qk], q='pool')
                            e1, e1k = e1r.next(); e2, e2k = e1r.next()
                            self.act(e1[:], Asrc[:], AF.Exp, [Akk], [e1k])
                            self.act(e2[:], Asrc[:], AF.Exp, [Akk], [e2k], scale=-1.0)
                            ki, kik = kir.next()
                            qz = [qzr[z].next() for z in range(2)]
                            for z in range(2):
                                self.tt(qz[z][0][z * 64:(z + 1) * 64, :], q[z * 64:(z + 1) * 64, :], e1[z * 64:(z + 1) * 64, :], ALU.mult, [qk, e1k], [qz[z][1]])
                            self.tt(ki[:], k[:], e2[:], ALU.mult, [kk, e2k], [kik])
                            qin[hp] = qz; kin[hp] = (ki, kik)
                        ptt, ptk = self.ps()
                        T.op('pe', lambda e: e.transpose(out=ptt[:, 0:128], in_=ko[:], identity=self.ident), [kok, 'cst'], [ptk])
                        for z in range(2):
                            self.act(kz[z][0][z * 64:(z + 1) * 64, hp, :], ptt[z * 64:(z + 1) * 64, 0:128], AF.Copy, [ptk], [kz[z][1]])
                    return (qin, kin, dtot, kz)

            def proc(t, d_, state_only, final, P):
                chunk_order = (0, 1) if d_ == 0 else (1, 0)
                qin, kin, dtot, kz = P
                r0 = t * 128
                for _once in (0,):
                    v, vk = vtr.next()
                    self.ld(v[:], self.gv[r0:r0 + 128, :], [], [vk], q='pool')
                    pkvs = [self.ps() for hp in range(2)]
                    for hp in range(2):
                        for c in range(2):
                            self.mm(pkvs[hp][0][:, c * 256:(c + 1) * 256], kz[c][0][:, hp, :], v[:, hp * 256:(hp + 1) * 256], True, True, [kz[c][1], vk], [pkvs[hp][1]])
                    if not state_only:
                        pat, patk = self.ps()
                        for h in range(4):
                            hp, hh = h // 2, h % 2
                            self.mm(pat[:, h * 128:(h + 1) * 128], kin[hp][0][:, :], qin[hp][hh][0][:, :],
                                    True, True, [kin[hp][1], qin[hp][hh][1]], [patk])
                        am, amk = amr.next()
                        self.tt(am[:], pat[:], mk[:, d_, :], ALU.mult, [patk, 'mk'], [amk])
                        po, pok = self.ps()
                    if not state_only:
                        s_first = [(S[hp][par[hp]], sk(hp)) for hp in range(2)]
                    c0 = chunk_order[0]
                    for hp in range(2):
                        nxt = 1 - par[hp]
                        for z in range(2):
                            col = c0 * 256 + z * 128
                            self.stt(S[hp][nxt][z * 64:(z + 1) * 64, :], S[hp][par[hp]][z * 64:(z + 1) * 64, :], dtot[hp][0][z * 64:(z + 1) * 64, c0:c0 + 1],
                                     pkvs[hp][0][z * 64:(z + 1) * 64, col:col + 128], ALU.mult, ALU.add,
                                     [sk(hp), dtot[hp][1], pkvs[hp][1]], [('S', hp, nxt)])
                    if not state_only:
                        s_mid = [(S[hp][1 - par[hp]], ('S', hp, 1 - par[hp])) for hp in range(2)]
                        for h in range(4):
                            hp, hh = h // 2, h % 2
                            self.mm(po[:, h * 128:(h + 1) * 128], v[:, h * 128:(h + 1) * 128], am[:, h * 128:(h + 1) * 128], True, False, [vk, amk], [pok])
                            for ci, c in enumerate(chunk_order):
                                s_t, s_k = (s_first if ci == 0 else s_mid)[hp]
                                self.mm(po[:, h * 128 + c * 64:h * 128 + (c + 1) * 64], s_t[:, :],
                                        qin[hp][hh][0][:, c * 64:(c + 1) * 64], False, ci == 1, [s_k, qin[hp][hh][1]], [pok])
                    c1 = chunk_order[1]
                    for hp in range(2):
                        cur = 1 - par[hp]
                        for z in range(2):
                            col = c1 * 256 + z * 128
                            self.stt(S[hp][par[hp]][z * 64:(z + 1) * 64, :], S[hp][cur][z * 64:(z + 1) * 64, :], dtot[hp][0][z * 64:(z + 1) * 64, c1:c1 + 1],
                                     pkvs[hp][0][z * 64:(z + 1) * 64, col:col + 128], ALU.mult, ALU.add,
                                     [('S', hp, cur), dtot[hp][1], pkvs[hp][1]], [sk(hp)])
                    if state_only:
                        continue
                    if not final:
                        self.act(oT[:, :, r0:r0 + 128], po[:].rearrange("p (h t) -> p h t", h=4), AF.Copy, [pok], [('oT', t)])
                    else:
                        os_, osk = osr.next()
                        self.tt(os_[:].rearrange("p (h t) -> p h t", h=4), po[:].rearrange("p (h t) -> p h t", h=4), oT[:, :, r0:r0 + 128], ALU.add, [pok, ('oT', t)], [osk])
                        sq, sqk = sqr.next()
                        self.act(sq[:], os_[:], AF.Square, [osk], [sqk])
                        pn, pnk = self.ps()
                        self.mm(pn[:], self.ones, sq[:], True, True, [sqk, 'cstr'], [pnk])
                        rs, rsk = rsr.next()
                        self.act(rs[:], pn[:], AF.Sqrt, [pnk], [rsk], bias=self.epsD[:, 0:1], scale=1.0 / 128)
                        T.op('dve', lambda e: e.reciprocal(out=rs[:], in_=rs[:]), [rsk], [rsk])
                        self.tt(rs[:], rs[:], os_[:], ALU.mult, [rsk, osk], [rsk])
                        og, ogk = ogr.next()
                        self.ld(og[:], self.ogT[:, :, r0:r0 + 128].rearrange("h p t -> p h t"), [], [ogk], q='pool')
                        bo, bok = bor.next()
                        self.stt(bo[:].rearrange("p h t -> p (h t)"), rs[:], self.smg[:, l, 0:1], og[:].rearrange("p h t -> p (h t)"), ALU.mult, ALU.mult, [rsk, ogk, 'smg'], [bok])
                        self.ld(self.mixT[4:8, :, r0:r0 + 128].rearrange("h p t -> p h t"), bo[:], [bok], [('mixT', 'b', t)], force='pool')

            need_ctx = (l == 0)
            if prescan:
                set_state(None, [])
                run([NT, NT + 1], 0, True, False)
                run(list(range(NT)), 0, True, False)
                for hp in range(2 if GLA_DBG >= 4 else 0):
                    self.ld(self.export[:, hp * 128:(hp + 1) * 128], S[hp][par[hp]][:], [sk(hp)], [('exp', hp)], q='pool')
                for kv in range(2 if GLA_DBG >= 5 else 0):
                    self.ld(self.export[:, 256 + kv * 128:256 + (kv + 1) * 128], self.KTs.bitcast(F32)[kv, :, TL - 128:TL], [], [('exp', 2 + kv)], q='pool')
                if GLA_DBG >= 5:
                    self.ld(self.export[:, 512:768], self.Vs.bitcast(F32)[TL - 128:TL, :], [], [('exp', 4)], q='pool')
            else:
                set_state(None, [])
                run([NT, NT + 1], 0, not need_ctx, False)
                run(list(range(NT)), 0, False, False)
                if MODE == 'full':
                    set_state(None, [])
                    run([NT + 1, NT], 1, not need_ctx, True)
                else:
                    if need_ctx:
                        set_state(None, [])
                        run([NT + 1, NT], 1, False, True)
                    set_state(self.partner, [])
                run(list(range(NT - 1, -1, -1)), 1, False, True)
            T.barrier()

    def att(self, l):
        T = self.T
        need_ctx = (l == 0)
        with ExitStack() as st:
            mk = self.sb(st, 'mk', [128, 3, 512])
            self.ld(mk[:], self.masks[:, 2:5, :], [], ['mk'])
            Kb = self.sb(st, 'Kb', [128, 2, NT + 3, 128], F32R); Vb = self.sb(st, 'Vb', [128, NT + 3, 256], F32R)
            es_ = self.sb(st, 'es', [128, 8])
            self.ld(es_[:], self.sinkb[:, l, :], [], ['es'])
            self.act(es_[:], es_[:], AF.Exp, ['es'], ['es'])
            for kv in range(2):
                self.ld(Kb[:, kv, 0:NT, :], self.KTs[kv, :, 0:TL].rearrange("p (b t) -> p b t", t=128), [], [('Kb', kv)])
                self.ld(Kb[:, kv, NT + 1:NT + 3, :], self.KTs[kv, :, TL:NTOK].rearrange("p (b t) -> p b t", t=128), [], [('Kb', kv)])
                if MODE == 'pair':
                    self.ld(Kb[:, kv, NT, :], self.partner.bitcast(F32R)[:, 256 + kv * 128:256 + (kv + 1) * 128], [], [('Kb', kv)])
            self.ld(Vb[:, 0:NT, :], self.Vs[0:TL, :].rearrange("(b p) c -> p b c", p=128), [], ['Vb'])
            self.ld(Vb[:, NT + 1:NT + 3, :], self.Vs[TL:NTOK, :].rearrange("(b p) c -> p b c", p=128), [], ['Vb'])
            if MODE == 'pair':
                self.ld(Vb[:, NT, :], self.partner.bitcast(F32R)[:, 512:768], [], ['Vb'])
            qr = Ring(self, st, 'qt', [128, 8, 128], F32R, 2)
            pr = Ring(self, st, 'pT', [128, 512], F32R, 12)
            dr = Ring(self, st, 'den', [128, 512], F32, 3)
            cor = Ring(self, st, 'co', [128, 4, 128], F32R, 3)
            qtiles = list(range(NT)) + ([NT + 1, NT + 2] if need_ctx else [])
            for qi in qtiles:
                tok0 = qi * 128 if qi < NT else TL + (qi - NT - 1) * 128
                qt, qk = qr.next()
                self.ld(qt[:], self.QT[:, :, tok0:tok0 + 128].rearrange("h p t -> p h t"), [], [qk], q='pool')
                cblk = [(NT + 1, None), (NT + 2, None)]
                if qi < NT:
                    nxt = [(qi + 1, 1)] if qi < NT - 1 else ([(NT, 2)] if MODE == 'pair' else [])
                    blocks = ([(qi - 1, 0)] if qi > 0 else []) + [(qi, None)] + nxt + cblk
                else:
                    blocks = cblk
                for kv in range(2):
                    pTs = []
                    for bi, (kb, m) in enumerate(blocks):
                        ps_, psk = self.ps()
                        self.mm(ps_[:], Kb[:, kv, kb, :], qt[:, kv * 4:(kv + 1) * 4, :].rearrange("p h t -> p (h t)"), True, True, [('Kb', kv), qk], [psk])
                        pT, pTk = pr.next()
                        self.act(pT[:], ps_[:], AF.Exp, [psk], [pTk], scale=128 ** -0.5)
                        if m is not None:
                            self.tt(pT[:], pT[:], mk[:, m, :], ALU.mult, [pTk, 'mk'], [pTk])
                        pTs.append((pT, pTk, kb))
                    po, pok = self.ps(); pd, pdk = self.ps()
                    for bi, (pT, pTk, kb) in enumerate(pTs):
                        self.mm(po[:], Vb[:, kb, kv * 128:(kv + 1) * 128], pT[:], bi == 0, bi == len(blocks) - 1, ['Vb', pTk], [pok])
                        self.mm(pd[:], self.ones, pT[:], bi == 0, bi == len(blocks) - 1, ['cstr', pTk], [pdk])
                    den, dk = dr.next()
                    for h in range(4):
                        self.ts(den[:, h * 128:(h + 1) * 128], pd[:, h * 128:(h + 1) * 128], es_[:, kv * 4 + h:kv * 4 + h + 1], None, ALU.add, None, [pdk, 'es'], [dk])
                    T.op('dve', lambda e: e.reciprocal(out=den[:], in_=den[:]), [dk], [dk])
                    co, cok = cor.next()
                    self.tt(co[:].rearrange("p h t -> p (h t)"), po[:], den[:], ALU.mult, [pok, dk], [cok])
                    self.ld(self.mixT[8 + kv * 4:12 + kv * 4, :, tok0:tok0 + 128].rearrange("h p t -> p h t"), co[:], [cok], [('mixT', 'c', qi, kv)], q='pool')
            T.barrier()

    def merge(self, l, tiles):
        with ExitStack() as st:
            self.eps_tile(st)
            hT = self.sb(st, 'hT', [128, KT, 512], F32R)
            mt = self.sb(st, 'mt', [128, KT, 512], F32R)
            mg = self.sb(st, 'mg', [128, KT, 512], F32R)
            R = {'xs': Ring(self, st, 'xs', [128, 512], F32, 3), 'sq': Ring(self, st, 'sq', [128, 512], F32R, 2),
                 'tmp': Ring(self, st, 'tmp', [128, 512], F32, 2), 'rstd': self.sb(st, 'rstd', [128, 512])}
            wf = Ring(self, st, 'wf', [128, KT * 128], F32R, 4)
            sgr = Ring(self, st, 'sg', [128, 512], F32, 4)
            m1r = Ring(self, st, 'm1', [128, 512], F32, 3)
            xo_r = Ring(self, st, 'xo', [128, 512], F32, 2)
            for (tok0, n) in tiles:
                wch = 1 if tok0 >= TL else 0
                self.norm_mod(R, self.xs, l, 1, wch, tok0, n, hT)
                self.ld(mt[:, :, :n], self.mixT[:, :, tok0:tok0 + n].rearrange("k p t -> p k t"), [], [('mt',)], q='pool')
                for nt in range(KT):
                    sgs = []
                    for b in range(3):
                        wt, wk = wf.next()
                        self.ld(wt[:], self.Wt('winF', l)[22 + b * 16 + nt, :, :], [], [wk])
                        pt, pk = self.ps()
                        for kt in range(KT):
                            self.mm(pt[:, :n], wt[:, kt * 128:(kt + 1) * 128], hT[:, kt, :n], kt == 0, kt == KT - 1, [wk, ('hT', kt, 0)], [pk])
                        sg, sk = sgr.next()
                        self.act(sg[:, :n], pt[:, :n], AF.Sigmoid, [pk], [sk])
                        sgs.append((sg, sk))
                    wt, wk = wf.next()
                    self.ld(wt[:], self.Wt('wbr', l)[nt, :, :], [], [wk])
                    ms = []
                    for b, (k0, k1) in enumerate(((0, 4), (4, 8), (8, 16))):
                        pt, pk = self.ps()
                        for kt in range(k0, k1):
                            self.mm(pt[:, :n], wt[:, kt * 128:(kt + 1) * 128], mt[:, kt, :n], kt == k0, kt == k1 - 1, [wk, ('mt',)], [pk])
                        m1, m1k = m1r.next()
                        self.tt(m1[:, :n], sgs[b][0][:, :n], pt[:, :n], ALU.mult, [sgs[b][1], pk], [m1k])
                        ms.append((m1, m1k))
                    self.tt(ms[0][0][:, :n], ms[0][0][:, :n], ms[1][0][:, :n], ALU.add, [ms[0][1], ms[1][1]], [ms[0][1]])
                    self.tt(mg[:, nt, :n], ms[0][0][:, :n], ms[2][0][:, :n], ALU.add, [ms[0][1], ms[2][1]], [('mg', nt)])
                for nt in range(KT):
                    wt, wk = wf.next()
                    self.ld(wt[:], self.Wt('wout', l)[nt, :, :], [], [wk])
                    pt, pk = self.ps()
                    for kt in range(KT):
                        self.mm(pt[:, :n], wt[:, kt * 128:(kt + 1) * 128], mg[:, kt, :n], kt == 0, kt == KT - 1, [wk, ('mg', kt)], [pk])
                    xt, xk = R['xs'].next()
                    self.ld(xt[:, :n], self.xs[nt, :, tok0:tok0 + n], [('X', nt, tok0)], [xk], q='pool')
                    xo, ok = xo_r.next()
                    self.stt(xo[:, :n], pt[:, :n], self.nG[:, l, 1, wch, nt:nt + 1], xt[:, :n], ALU.mult, ALU.add, [pk, xk, 'nG'], [ok])
                    self.ld(self.xs[nt, :, tok0:tok0 + n], xo[:, :n], [ok], [('X', nt, tok0)], force='act')
            self.T.barrier()

    def part_a(self, l, src):
        self.ffn(l, 0, src, self.xs, TILES)
        if DEBUG_STOP == 'ffn1':
            return
        self.inproj(l, TILES, ('bk', 'bv', 'ck', 'cv', 'dec'))
        if DEBUG_STOP == 'inproj_a':
            return
        self.gla(l, True)

    def part_b(self, l, last):
        self.inproj(l, TILES, ('bk', 'bq', 'au', 'bg', 'ck', 'cq', 'dec', 'bv', 'cv', 'av'))
        if DEBUG_STOP == 'b_inproj':
            return
        self.mixa(l, NT + 2 if not last else NT)
        if DEBUG_STOP == 'b_mixa':
            return
        self.gla(l, False)
        if DEBUG_STOP == 'b_gla':
            return
        self.att(l)
        if DEBUG_STOP == 'b_att':
            return
        tiles = TILES[:-1] if last else TILES
        self.merge(l, tiles)
        if DEBUG_STOP == 'b_merge':
            return
        if last:
            self.ffn(l, 1, self.xs, self.y, tiles)
        else:
            self.ffn(l, 1, self.xs, self.xs, tiles)

    def build(self):
        self.setup()
        self.mods()
        L = self.launch
        if L == 'full':
            for l in range(NL):
                self.ffn(l, 0, self.x_in if l == 0 else self.xs, self.xs, TILES)
                self.part_b(l, l == NL - 1)
        elif L == 0:
            self.part_a(0, self.x_in)
        elif L == 1:
            self.copy_x()
            self.part_b(0, False)
            self.part_a(1, self.xs)
        else:
            self.copy_x()
            self.part_b(1, True)
        self.T.barrier()
        self.es.close()
        return self.nc

    def copy_x(self):
        for kt in range(KT):
            self.ld(self.xs[kt, :, :], self.x_in[kt, :, :], [], [('Xc', kt)], q='pool')
        self.T.barrier()


def _tile_fm(w, c0, ntile):
    K = w.shape[0]
    sub = w[:, c0:c0 + ntile * 128].reshape(K // 128, 128, ntile, 128)
    return np.ascontiguousarray(sub.transpose(2, 1, 0, 3)).reshape(ntile, 128, (K // 128) * 128)


def _prep_weights(inp):
    W = {}
    f32 = np.float32
    for l in range(NL):
        W[f'wada{l}'] = _tile_fm(inp['w_ada'][l], 0, 144)
        for f in range(2):
            w = inp['w_ffn_up'][l, f]
            wu = np.empty((FT, 128, 2, KT * 128), f32)
            wu[:, :, 0, :] = _tile_fm(w, 0, FT)
            wu[:, :, 1, :] = _tile_fm(w, FF, FT)
            W[f'wup{l}{f}'] = wu
            wd = inp['w_ffn_down'][l, f].reshape(4, 11, 128, KT, 128)
            W[f'wdn{l}{f}'] = np.ascontiguousarray(wd.transpose(3, 0, 2, 1, 4)).reshape(KT, 4, 128, 11 * 128)
        w = inp['w_in'][l]
        winF = np.empty((70, 128, KT * 128), f32)
        for nm, (t0, cnt) in FM.items():
            winF[t0:t0 + cnt] = _tile_fm(w, COLS[nm], cnt)
        W[f'winF{l}'] = winF
        tm = np.concatenate([w[:, 256:768], w[:, 1024:1280], w[:, 1792:2304]], axis=1)
        W[f'winT{l}'] = np.ascontiguousarray(tm.reshape(KT, 128, 1280).transpose(1, 0, 2))
        cat = np.concatenate([inp['w_br_a'][l], inp['w_br_b'][l], inp['w_br_c'][l]], axis=0)
        W[f'wbr{l}'] = _tile_fm(cat, 0, KT)
        W[f'wout{l}'] = _tile_fm(inp['w_out'][l], 0, KT)
    W['badaT'] = np.ascontiguousarray(inp['b_ada'].reshape(NL, 144, 128).transpose(2, 0, 1))
    W['normg'] = np.ascontiguousarray(inp['norm_g'].reshape(NL, 3, KT, 128).transpose(3, 0, 1, 2))
    W['avg'] = np.ascontiguousarray(np.broadcast_to(inp['a_v_gain'][None], (128, NL, 512)))
    sm = np.zeros((128, NL, 4), f32)
    sm[:, :, 0] = inp['b_norm_g'].T; sm[:, :, 1] = inp['c_q_gain'].T; sm[:, :, 2] = inp['c_k_gain'].T
    W['smallg'] = sm
    W['sinkb'] = np.ascontiguousarray(np.broadcast_to(inp['c_sink'][None], (128, NL, 8)))
    return W


def _prep_core_weights(inp, half):
    f32 = np.float32
    W = {}
    dirs = (0, 1) if half == 0 else (1, 0)
    bdec = np.zeros((128, NL, 2, 2), f32)
    ws = inp['a_ws']; bs = inp['a_bs']
    if half == 1:
        ws = ws[:, :, ::-1, ::-1]; bs = bs[:, :, ::-1]
    for l in range(NL):
        pad = np.zeros((D, 128), f32)
        wdec2 = np.zeros((128, 2, 256), f32)
        for di, dg in enumerate(dirs):
            pad[:, di * 16:(di + 1) * 16] = inp['b_decay_w1'][l, dg]
            wdec2[di * 16:(di + 1) * 16, di, :] = inp['b_decay_w2'][l, dg]
            bdec[:, l, di, :] = inp['b_decay_b'][l, dg].reshape(2, 128).T
        W[f'wdec1{l}'] = _tile_fm(pad, 0, 1)[0]
        W[f'wdec2{l}'] = wdec2
        W[f'awsT{l}'] = np.ascontiguousarray(ws[l].transpose(2, 0, 1))
    W['bdec'] = bdec
    W['absb'] = np.ascontiguousarray(np.broadcast_to(bs.reshape(NL, 512)[None], (128, NL, 512)))
    return W


def _consts():
    c = np.zeros((128, 6, 128), np.float32)
    c[:, 0, :] = 1.0
    c[:, 1, :] = np.eye(128, dtype=np.float32)
    Pm = np.zeros((128, 128), np.float32)
    for base in (0, 64):
        for d in range(32):
            Pm[base + d, base + d + 32] = -1.0
            Pm[base + 32 + d, base + d] = 1.0
    c[:, 2, :] = Pm.T
    rm = np.ones((128, 128), np.float32); rm[:, 0] = 0.0; rm[:, 64] = 0.0
    c[:, 3, :] = rm
    m = np.zeros((128, 5, 512), np.float32)
    s = np.arange(128)[:, None]; t = np.arange(128)[None, :]
    same = (s // 64) == (t // 64)
    tile4 = lambda a: np.tile(a.astype(np.float32), (1, 4))
    m[:, 0, :] = tile4(same & (s <= t)); m[:, 1, :] = tile4(same & (s >= t))
    m[:, 2, :] = tile4(s >= t); m[:, 3, :] = tile4(s <= t); m[:, 4, :] = tile4(s >= 127 - t)
    return c, m


def _rope_tables(half):
    pos = np.arange(TL) + (0 if half == 0 else TL)
    if half == 1:
        pos = pos[::-1]
    row = (pos // 64).astype(np.float32); col = (pos % 64).astype(np.float32)
    inv = (10000.0 ** (-np.arange(32, dtype=np.float32) / 32)).astype(np.float32)
    tab = np.zeros((128, 2, NTOK), np.float32)
    tab[:, 0, TL:] = 1.0
    for d in range(128):
        p = row if d < 64 else col
        ang = p * inv[d % 32]
        tab[d, 0, :TL] = np.cos(ang); tab[d, 1, :TL] = np.sin(ang)
    return tab


def _core_maps(inp, Wc, cst, msk, cores):
    per_half = {}
    ropes = {}
    maps = []
    for (b, h) in cores:
        hh = 0 if h is None else h
        if hh not in per_half:
            per_half[hh] = _prep_core_weights(inp, hh); ropes[hh] = _rope_tables(hh)
        if h is None:
            lat = inp['x'][b]; cx = inp['ctx'][b]
        else:
            lat = inp['x'][b, h * TL:(h + 1) * TL]; cx = inp['ctx'][b]
            if h == 1:
                lat = lat[::-1]; cx = cx[::-1]
        xl = np.concatenate([lat, cx], axis=0)
        m = dict(Wc); m.update(per_half[hh])
        m['consts'] = cst; m['masks'] = msk; m['rope'] = ropes[hh]
        cT = np.stack([inp['c'][b], inp['c_ctx']], axis=1).reshape(KT, 128, 2).transpose(1, 0, 2)
        m['cT'] = np.ascontiguousarray(cT)
        m['xT'] = np.ascontiguousarray(xl.T).reshape(KT, 128, NTOK)
        maps.append(m)
    return maps


def kernel(**inp):
    inp = {k: np.asarray(v, dtype=np.float32) for k, v in inp.items()}
    Wc = _prep_weights(inp)
    cst, msk = _consts()
    out = np.empty((4, 4096, D), np.float32)
    if MODE == 'full':
        base_maps = _core_maps(inp, Wc, cst, msk, [(b, None) for b in range(4)])
        kb = KB('full')
        nc = kb.build()
        maps = [{k: m[k] for k in kb.din} for m in base_maps]
        res = run_bass_kernel_spmd(nc, maps, core_ids=list(range(4)))
        for b in range(4):
            out[b] = np.asarray(res.results[b]['yT']).reshape(D, TL).T
        return out
    base_maps = _core_maps(inp, Wc, cst, msk, [(r // 2, r % 2) for r in range(NCORE)])
    state = None
    for launch in range(3):
        kb = KB(launch)
        nc = kb.build()
        maps = []
        for r in range(NCORE):
            src = dict(base_maps[r])
            if launch > 0:
                src['xs_in'] = state[r]['xs']; src['mod_in'] = state[r]['mod']; src['partner'] = state[r ^ 1]['export']
            maps.append({k: src[k] for k in kb.din})
        res = run_bass_kernel_spmd(nc, maps, core_ids=list(range(NCORE)))
        if launch < 2:
            new = []
            for r in range(NCORE):
                o = res.results[r]
                new.append({'xs': np.asarray(o['xs']), 'export': np.asarray(o['export']),
                            'mod': np.asarray(o['mod_out']) if launch == 0 else state[r]['mod']})
            state = new
    for r in range(NCORE):
        b, h = r // 2, r % 2
        y = np.asarray(res.results[r]['yT']).reshape(D, TL).T
        if h == 1:
            y = y[::-1]
        out[b, h * TL:(h + 1) * TL] = y
    return out
```
